# Optimizing a Trainium2 kernel written in Bass

```python
import math
import jax, jax.numpy as jnp
from jax import lax
import numpy as np

D_MODEL = 2048
BATCH = 4
SEQ = 2048
DEPTH = 1

MEM_LEN = 256
HEAD_DIM = 64
CONV_DIM = 3 * D_MODEL // 8
CONV_WIDTH = 3
SWA_HEADS = 3 * D_MODEL // (8 * HEAD_DIM)
SWA_KV_HEADS = 4
SWA_GROUP = SWA_HEADS // SWA_KV_HEADS
SWA_DIM = SWA_HEADS * HEAD_DIM
KV_DIM = SWA_KV_HEADS * HEAD_DIM
WINDOW = 128
BLOCK = 128
MEM_HEADS = 4
MEM_DIM = D_MODEL // 4
MEM_HEAD_DIM = MEM_DIM // MEM_HEADS
MIX_DIM = CONV_DIM + SWA_DIM + MEM_DIM
PROJ_DIM = 3 * CONV_DIM + SWA_DIM + 2 * KV_DIM + MEM_DIM
D_FF = 256 * (-(-(8 * D_MODEL // 3) // 256))
N_BUCKETS = 32
MAX_DISTANCE = 128
ALPHA = (2.0 * DEPTH) ** 0.25
BETA = (8.0 * DEPTH) ** -0.25
LN_EPS = 1e-5
NEG_INF = -1e30

kernel_name = "hymba_conv_swa_mem_macaron_deepnorm"


def _layer_norm(x, g, b):
    xf = x.astype(jnp.float32)
    mu = xf.mean(-1, keepdims=True)
    var = jnp.square(xf - mu).mean(-1, keepdims=True)
    return ((xf - mu) * lax.rsqrt(var + LN_EPS) * g + b).astype(x.dtype)


def _swiglu(x, w_gate, w_up, w_down):
    return (jax.nn.silu(x @ w_gate) * (x @ w_up)) @ w_down


def _short_conv(u, w):
    s = u.shape[1]
    up = jnp.pad(u, ((0, 0), (CONV_WIDTH - 1, 0), (0, 0)))
    y = w[CONV_WIDTH - 1] * u
    for j in range(CONV_WIDTH - 1):
        y = y + w[j] * up[:, j:j + s]
    return y


def _t5_bucket(dist):
    max_exact = N_BUCKETS // 2
    d = jnp.maximum(dist, 1).astype(jnp.float32)
    large = max_exact + (jnp.log(d / max_exact) / math.log(MAX_DISTANCE / max_exact)
                         * (N_BUCKETS - max_exact)).astype(jnp.int32)
    large = jnp.minimum(large, N_BUCKETS - 1)
    return jnp.where(dist < max_exact, dist, large)


def _sliding_window_attention(q, k, v, rel_bias, sinks):
    b, s = q.shape[:2]
    nb = s // BLOCK
    qb = q.reshape(b, nb, BLOCK, SWA_KV_HEADS, SWA_GROUP, HEAD_DIM)

    def band(t):
        tb = t.reshape(b, nb, BLOCK, SWA_KV_HEADS, HEAD_DIM)
        prev = jnp.pad(tb, ((0, 0), (1, 0), (0, 0), (0, 0), (0, 0)))[:, :-1]
        return jnp.concatenate([prev, tb], axis=2)

    kb, vb = band(k), band(v)
    logits = jnp.einsum('bnqhgd,bnkhd->bnhgqk', qb, kb).astype(jnp.float32) * (HEAD_DIM ** -0.5)

    qi = jnp.arange(BLOCK)[:, None]
    kj = jnp.arange(2 * BLOCK)[None, :]
    dist = qi + BLOCK - kj
    in_window = (dist >= 0) & (dist < WINDOW)
    key_valid = (jnp.arange(nb)[:, None] * BLOCK - BLOCK + kj) >= 0
    mask = in_window[None] & key_valid[:, None, :]

    bias = rel_bias[_t5_bucket(jnp.maximum(dist, 0))]
    bias = bias.transpose(2, 0, 1).reshape(SWA_KV_HEADS, SWA_GROUP, BLOCK, 2 * BLOCK)
    logits = jnp.where(mask[None, :, None, None], logits + bias.astype(jnp.float32), NEG_INF)

    sink = sinks.astype(jnp.float32).reshape(1, 1, SWA_KV_HEADS, SWA_GROUP, 1, 1)
    m = jnp.maximum(logits.max(-1, keepdims=True), sink)
    p = jnp.exp(logits - m)
    probs = p / (p.sum(-1, keepdims=True) + jnp.exp(sink - m))
    out = jnp.einsum('bnhgqk,bnkhd->bnqhgd', probs.astype(vb.dtype), vb)
    return out.reshape(b, s, SWA_DIM)


def _memory_attention(qm, mem_k, mem_v):
    b, s = qm.shape[:2]
    q = qm.reshape(b, s, MEM_HEADS, MEM_HEAD_DIM)
    k = mem_k.reshape(b, MEM_LEN, MEM_HEADS, MEM_HEAD_DIM)
    v = mem_v.reshape(b, MEM_LEN, MEM_HEADS, MEM_HEAD_DIM)
    logits = jnp.einsum('bshd,bmhd->bhsm', q, k).astype(jnp.float32) * (MEM_HEAD_DIM ** -0.5)
    p = jax.nn.softmax(logits, axis=-1)
    return jnp.einsum('bhsm,bmhd->bshd', p.astype(v.dtype), v).reshape(b, s, MEM_DIM)


def setup_inputs(seed: int = 0) -> dict:
    key = jax.random.key(seed)
    ks = jax.random.split(key, 24)
    f32 = jnp.float32
    nrm = lambda k, shape, scale: jax.random.normal(k, shape, f32) * scale
    L = DEPTH
    col_scale = jnp.concatenate([
        jnp.ones((2 * CONV_DIM,), f32), jnp.full((CONV_DIM,), BETA, f32),
        jnp.ones((SWA_DIM + KV_DIM,), f32), jnp.full((KV_DIM,), BETA, f32),
        jnp.ones((MEM_DIM,), f32)])
    mem_scale = jnp.concatenate([jnp.ones((MEM_DIM,), f32), jnp.full((MEM_DIM,), BETA, f32)])
    return {
        "x": nrm(ks[0], (BATCH, SEQ, D_MODEL), 1.0),
        "mem": nrm(ks[1], (BATCH, MEM_LEN, D_MODEL), 1.0),
        "ln1_g": 1.0 + nrm(ks[2], (L, D_MODEL), 0.01),
        "ln1_b": nrm(ks[3], (L, D_MODEL), 0.01),
        "ffn1_w_gate": nrm(ks[4], (L, D_MODEL, D_FF), BETA * D_MODEL ** -0.5),
        "ffn1_w_up": nrm(ks[5], (L, D_MODEL, D_FF), BETA * D_MODEL ** -0.5),
        "ffn1_w_down": nrm(ks[6], (L, D_FF, D_MODEL), BETA * D_FF ** -0.5),
        "w_in": nrm(ks[7], (L, D_MODEL, PROJ_DIM), D_MODEL ** -0.5) * col_scale,
        "b_in": nrm(ks[8], (L, PROJ_DIM), 0.01),
        "conv_w": nrm(ks[9], (L, CONV_WIDTH, CONV_DIM), CONV_WIDTH ** -0.5),
        "sinks": nrm(ks[10], (L, SWA_HEADS), 0.5),
        "w_mem_kv": nrm(ks[11], (L, D_MODEL, 2 * MEM_DIM), D_MODEL ** -0.5) * mem_scale,
        "w_out": nrm(ks[12], (L, MIX_DIM, D_MODEL), BETA * MIX_DIM ** -0.5),
        "ln2_g": 1.0 + nrm(ks[13], (L, D_MODEL), 0.01),
        "ln2_b": nrm(ks[14], (L, D_MODEL), 0.01),
        "ffn2_w_gate": nrm(ks[15], (L, D_MODEL, D_FF), BETA * D_MODEL ** -0.5),
        "ffn2_w_up": nrm(ks[16], (L, D_MODEL, D_FF), BETA * D_MODEL ** -0.5),
        "ffn2_w_down": nrm(ks[17], (L, D_FF, D_MODEL), BETA * D_FF ** -0.5),
        "ln3_g": 1.0 + nrm(ks[18], (L, D_MODEL), 0.01),
        "ln3_b": nrm(ks[19], (L, D_MODEL), 0.01),
        "rel_bias": nrm(ks[20], (N_BUCKETS, SWA_HEADS), 0.2),
    }


def reference(x, mem, ln1_g, ln1_b, ffn1_w_gate, ffn1_w_up, ffn1_w_down, w_in, b_in,
              conv_w, sinks, w_mem_kv, w_out, ln2_g, ln2_b, ffn2_w_gate, ffn2_w_up,
              ffn2_w_down, ln3_g, ln3_b, rel_bias):
    splits = list(np.cumsum([CONV_DIM, CONV_DIM, CONV_DIM, SWA_DIM, KV_DIM, KV_DIM]))
    h = x
    for l in range(DEPTH):
        h = _layer_norm(ALPHA * h + 0.5 * _swiglu(h, ffn1_w_gate[l], ffn1_w_up[l], ffn1_w_down[l]),
                        ln1_g[l], ln1_b[l])
        proj = h @ w_in[l] + b_in[l]
        b_gate, c_gate, u, q, k, v, qm = jnp.split(proj, splits, axis=-1)
        conv_out = b_gate * _short_conv(c_gate * u, conv_w[l])
        swa_out = _sliding_window_attention(q, k, v, rel_bias, sinks[l])
        mem_k, mem_v = jnp.split(mem @ w_mem_kv[l], 2, axis=-1)
        mem_out = _memory_attention(qm, mem_k, mem_v)
        mix = jnp.concatenate([conv_out, swa_out, mem_out], axis=-1) @ w_out[l]
        h = _layer_norm(ALPHA * h + mix, ln2_g[l], ln2_b[l])
        h = _layer_norm(ALPHA * h + 0.5 * _swiglu(h, ffn2_w_gate[l], ffn2_w_up[l], ffn2_w_down[l]),
                        ln3_g[l], ln3_b[l])
    return h
```

```python
import math
from contextlib import ExitStack

import numpy as np
import concourse.bass as bass
import concourse.mybir as mybir
from concourse.bass_utils import run_bass_kernel_spmd

F32 = mybir.dt.float32
BF16 = mybir.dt.bfloat16
AF = mybir.ActivationFunctionType
ALU = mybir.AluOpType

D = 2048
DFF = 5632
NKC = 16
NPART = DFF // 256
TOWN = 1024
HALO = 128
T = TOWN + HALO
SEQ = 2048
BATCH = 4
MEM_LEN = 256
ALPHA = 2.0 ** 0.25
LN_EPS = 1e-5
NEG = -30000.0
TG_OWN = [(128, 512), (640, 512)]
TG_ALL = [(0, 384), (384, 384), (768, 384)]
NS = 432
C_LN = 0
C_BIN = 96
C_BQ = 128
C_BK = 140
C_CW = 144
C_SINK = 162
C_FLAG = 174
C_EPS = 175
C_BV = 176


class Sem:
    def __init__(self, h):
        self.h = h
        self.count = 0


class Eng:
    def __init__(self, name, h, sem=None):
        self.name = name
        self.h = h
        self.sem = sem
        self.seen = {}


class Bank:
    def __init__(self, ap, key):
        self.ap = ap
        self.key = key


class Ctx:
    def __init__(self, nc, es):
        self.nc = nc
        self.es = es
        self.regions = {}
        self.sems = []
        self.bank_i = 0
        self.trace = {}
        self.dsems = {}

    def new_sem(self, name):
        s = Sem(self.es.enter_context(self.nc.semaphore(name)))
        self.sems.append(s)
        return s

    def _wait(self, eng, tok):
        sem, val = tok
        if eng.seen.get(sem, 0) >= val:
            return
        eng.h.wait_ge(sem.h, val)
        eng.seen[sem] = val
        self.trace.setdefault(eng.name, []).append(("w", id(sem), val))

    def deps(self, eng, reads, writes, own=None):
        own = own if own is not None else eng.sem
        for k in reads:
            r = self.regions.get(k)
            if r is not None and r[0] is not None:
                self._wait(eng, r[0])
        for k in writes:
            r = self.regions.get(k)
            if r is not None:
                if r[0] is not None and r[0][0] is not own:
                    self._wait(eng, r[0])
                for s, v in r[1].items():
                    if s is not own:
                        self._wait(eng, (s, v))

    def commit(self, tok, reads, writes):
        for k in writes:
            self.regions[k] = [tok, {}]
        for k in reads:
            r = self.regions.setdefault(k, [None, {}])
            if r[1].get(tok[0], 0) < tok[1]:
                r[1][tok[0]] = tok[1]

    def op(self, eng, fn, reads=(), writes=()):
        self.deps(eng, reads, writes)
        ins = fn()
        ins.then_inc(eng.sem.h, 1)
        eng.sem.count += 1
        self.trace.setdefault(eng.name, []).append(("i", id(eng.sem), 1))
        self.commit((eng.sem, eng.sem.count), reads, writes)

    def mm_group(self, out, pairs, reads, writes):
        pe = self.PE
        self.deps(pe, reads, writes)
        n = len(pairs)
        ins = None
        for i, (l, r) in enumerate(pairs):
            ins = self.nc.tensor.matmul(out, lhsT=l, rhs=r, start=(i == 0), stop=(i == n - 1))
        ins.then_inc(pe.sem.h, 1)
        pe.sem.count += 1
        self.trace.setdefault(pe.name, []).append(("i", id(pe.sem), 1))
        self.commit((pe.sem, pe.sem.count), reads, writes)

    def dma(self, q, dsem, out, in_, reads=(), writes=()):
        if isinstance(dsem, str):
            if dsem not in self.dsems:
                self.dsems[dsem] = self.new_sem("d_" + dsem)
            dsem = self.dsems[dsem]
        self.deps(q, reads, writes, own=dsem)
        q.h.dma_start(out=out, in_=in_).then_inc(dsem.h, 16)
        dsem.count += 16
        self.trace.setdefault(q.name, []).append(("i", id(dsem), 16))
        self.commit((dsem, dsem.count), reads, writes)

    def barrier(self):
        for e in self.engines:
            for s in self.sems:
                if s.count > 0:
                    self._wait(e, (s, s.count))
        self.regions = {}

    def bank(self):
        i = self.bank_i
        self.bank_i = (i + 1) % 8
        return Bank(self.PS[i // 2][:, i % 2, :], f"ps{i}")

    def bank2(self):
        if self.bank_i % 2:
            self.bank_i = (self.bank_i + 1) % 8
        i = self.bank_i
        self.bank_i = (i + 2) % 8
        return self.PS[i // 2], [f"ps{i}", f"ps{i + 1}"]


def build_program():
    nc = bass.Bass("TRN2", target_bir_lowering=False)
    es = ExitStack()
    cx = Ctx(nc, es)

    def din(name, shape):
        return nc.dram_tensor(name, shape, F32, kind="ExternalInput").ap()

    xT = din("xT", [D, T])
    memT = din("memT", [D, MEM_LEN])
    smalls_d = din("smalls", [128, NS])
    biasT_d = din("biasT", [128, 2 * 12 * 128])
    bias0T_d = din("bias0T", [128, 12 * 128])
    ident_d = din("ident", [128, 128])
    w1g = din("w1g", [D, DFF]); w1u = din("w1u", [D, DFF]); w1d = din("w1d", [DFF, D])
    w2g = din("w2g", [D, DFF]); w2u = din("w2u", [D, DFF]); w2d = din("w2d", [DFF, D])
    w_in = din("w_in", [D, 4096])
    w_mkv = din("w_mkv", [D, 1024])
    w_out = din("w_out", [D, D])
    outT = nc.dram_tensor("outT", [D, TOWN], F32, kind="ExternalOutput").ap()

    def sb(stack, name, shape, dt=F32):
        return stack.enter_context(nc.sbuf_tensor(name, shape, dt))

    Z = sb(es, "Z", [128, NKC, T], F32)
    HB = sb(es, "HB", [128, NKC, T], BF16)
    SM = sb(es, "SM", [128, NS], F32)
    DER = sb(es, "DER", [128, 32], F32)
    DER2 = sb(es, "DER2", [128, 64], F32)
    ONES = sb(es, "ONES", [128, 128], F32)
    ONESB = sb(es, "ONESB", [128, 128], BF16)
    cx.PS = [es.enter_context(nc.psum_tensor(f"ps{i}", [128, 2, 512], F32)) for i in range(4)]

    pe_sem = cx.new_sem("s_pe"); act_sem = cx.new_sem("s_act"); dve_sem = cx.new_sem("s_dve")
    pool_sem = cx.new_sem("s_pool")
    osem = cx.new_sem("s_o")
    PE = Eng("pe", nc.tensor, pe_sem); ACT = Eng("act", nc.scalar, act_sem); DVE = Eng("dve", nc.vector, dve_sem)
    PQ = Eng("pool", nc.gpsimd, pool_sem); SQ_ = Eng("spq", nc.sync, None)
    POOL = PQ
    cx.PE = PE
    cx.engines = [PE, ACT, DVE, PQ, SQ_]

    ov = outT.rearrange("(kc p) t -> p kc t", p=128)

    cx.dma(SQ_, "sm", out=SM[:], in_=smalls_d, writes=["sm"])
    xv = xT.rearrange("(kc p) t -> p kc t", p=128)
    for h in range(2):
        cx.dma(PQ, "hbinit", out=HB[:, h * 8:(h + 1) * 8, :], in_=xv[:, h * 8:(h + 1) * 8, :], writes=["hbinit"])
    for h in range(2):
        cx.dma(SQ_, f"zinit{h}", out=Z[:, h * 8:(h + 1) * 8, :], in_=xv[:, h * 8:(h + 1) * 8, :],
               writes=[f"z{d}_{ti}" for d in range(h * 8, (h + 1) * 8) for ti in range(3)])
    cx.op(DVE, lambda: nc.vector.memset(ONES[:], 1.0), writes=["ones"])
    cx.op(DVE, lambda: nc.vector.memset(ONESB[:], 1.0), writes=["onesb"])
    cx.op(DVE, lambda: nc.vector.tensor_scalar(out=DER[0:64, 0:12], in0=SM[0:64, C_BQ:C_BQ + 12], scalar1=0.125,
                                               scalar2=None, op0=ALU.mult), reads=["sm"], writes=["der0"])
    cx.op(DVE, lambda: nc.vector.tensor_scalar(out=DER2[:], in0=SM[:, 0:64], scalar1=ALPHA, scalar2=None,
                                               op0=ALU.mult), reads=["sm"], writes=["der3"])
    cx.op(DVE, lambda: nc.vector.tensor_scalar(out=DER[:, 12:16], in0=SM[:, C_BIN + 28:C_BIN + 32],
                                               scalar1=128.0 ** -0.5, scalar2=None, op0=ALU.mult),
          reads=["sm"], writes=["der1"])
    cx.op(ACT, lambda: nc.scalar.activation(out=DER[:, 16:28], in_=SM[:, C_SINK:C_SINK + 12], func=AF.Exp),
          reads=["sm"], writes=["der2"])

    class FFN:
        def __init__(self, tag, wg, wu, wd, tgs, hb_keys, ln_prev=None):
            self.ln_prev = ln_prev
            self.st = ExitStack()
            st = self.st
            self.tgs = tgs
            self.hb_keys = hb_keys
            self.WGU = [sb(st, f"{tag}wgu{i}", [128, 2, NKC, 256], BF16) for i in range(2)]
            self.WD = [sb(st, f"{tag}wd{i}", [128, 2, D], BF16) for i in range(2)]
            self.ACTB = [sb(st, f"{tag}act{i}", [128, 2, T], BF16) for i in range(2)]
            self.SCR = [sb(st, f"{tag}scr{i}", [128, 512], F32) for i in range(2)]
            self.wgv = wg.rearrange("(kc p) f -> p kc f", p=128)
            self.wuv = wu.rearrange("(kc p) f -> p kc f", p=128)
            self.wdv = wd.rearrange("(fc p) d -> p fc d", p=128)
            self.scr_i = 0

        def load_gu(self, p):
            s = p % 2
            cx.dma(PQ, f"wg{s}", out=self.WGU[s][:, 0], in_=self.wgv[:, :, p * 256:(p + 1) * 256], writes=[f"wg{s}"])
            cx.dma(PQ, f"wu{s}", out=self.WGU[s][:, 1], in_=self.wuv[:, :, p * 256:(p + 1) * 256], writes=[f"wu{s}"])

        def load_d(self, p):
            s = p % 2
            cx.dma(PQ, f"wd{s}", out=self.WD[s][:], in_=self.wdv[:, 2 * p:2 * p + 2, :], writes=[f"wd{s}"])

        def prefetch(self):
            self.load_gu(0); self.load_d(0); self.load_gu(1); self.load_d(1)

        def A(self, p):
            s = p % 2
            WGU, ACTB, SCR = self.WGU, self.ACTB, self.SCR
            for fi in range(2):
                for ti, (t0, tl) in enumerate(self.tgs):
                    bg = cx.bank(); bu = cx.bank()
                    cx.mm_group(bg.ap[:, 0:tl],
                                [(WGU[s][:, 0, kc, fi * 128:(fi + 1) * 128], HB[:, kc, t0:t0 + tl]) for kc in range(NKC)],
                                reads=[f"wg{s}"] + self.hb_keys(ti), writes=[bg.key])
                    cx.mm_group(bu.ap[:, 0:tl],
                                [(WGU[s][:, 1, kc, fi * 128:(fi + 1) * 128], HB[:, kc, t0:t0 + tl]) for kc in range(NKC)],
                                reads=[f"wu{s}"] + self.hb_keys(ti), writes=[bu.key])
                    c = self.scr_i; self.scr_i = 1 - c
                    cx.op(ACT, lambda: nc.scalar.activation(out=SCR[c][:, 0:tl], in_=bg.ap[:, 0:tl], func=AF.Silu),
                          reads=[bg.key], writes=[f"scr{c}"])
                    cx.op(DVE, lambda: nc.vector.scalar_tensor_tensor(
                        out=ACTB[s][:, fi, t0:t0 + tl], in0=bu.ap[:, 0:tl], scalar=0.5, in1=SCR[c][:, 0:tl],
                        op0=ALU.mult, op1=ALU.mult), reads=[bu.key, f"scr{c}"], writes=[f"act{s}_{fi}_{ti}"])

        def B(self, p):
            s = p % 2
            WD, ACTB = self.WD, self.ACTB
            for d in range(NKC):
                for ti, (t0, tl) in enumerate(self.tgs):
                    b = cx.bank()
                    cx.mm_group(b.ap[:, 0:tl],
                                [(WD[s][:, fi, d * 128:(d + 1) * 128], ACTB[s][:, fi, t0:t0 + tl]) for fi in range(2)],
                                reads=[f"wd{s}", f"act{s}_0_{ti}", f"act{s}_1_{ti}"], writes=[b.key])
                    zk = f"z{d}_{ti}"
                    zap = Z[:, d, t0:t0 + tl]
                    lp = self.ln_prev
                    if p == 0:
                        sc = ALPHA if lp is None else DER2[:, 32 * lp + d:32 * lp + d + 1]
                        cx.op(DVE, lambda: nc.vector.scalar_tensor_tensor(
                            out=zap, in0=zap, scalar=sc, in1=b.ap[:, 0:tl], op0=ALU.mult, op1=ALU.add),
                            reads=[b.key, zk, "der3"], writes=[zk])
                    elif p == 1 and lp is not None:
                        cx.op(DVE, lambda: nc.vector.scalar_tensor_tensor(
                            out=zap, in0=b.ap[:, 0:tl], scalar=DER2[:, 32 * lp + 16 + d:32 * lp + 17 + d], in1=zap,
                            op0=ALU.add, op1=ALU.add), reads=[b.key, zk, "der3"], writes=[zk])
                    else:
                        cx.op(DVE, lambda: nc.vector.tensor_tensor(out=zap, in0=b.ap[:, 0:tl], in1=zap, op=ALU.add),
                              reads=[b.key, zk], writes=[zk])

        def run(self):
            self.A(0)
            for p in range(NPART):
                if p + 1 < NPART:
                    self.A(p + 1)
                if p + 2 < NPART:
                    self.load_gu(p + 2)
                self.B(p)
                if p + 2 < NPART:
                    self.load_d(p + 2)
            cx.barrier()
            self.st.close()

    def layernorm(tag, tgs, ln_idx, final):
        gc = C_LN + ln_idx * 32
        bc = gc + 16
        with ExitStack() as st:
            SQ = [sb(st, f"{tag}sq{i}", [128, 512], F32) for i in range(2)]
            TMP = [sb(st, f"{tag}tmp{i}", [128, 512], F32) for i in range(2)]
            MEAN = [sb(st, f"{tag}mean{i}", [128, 512], F32) for i in range(2)]
            RSTD = [sb(st, f"{tag}rstd{i}", [128, 512], F32) for i in range(2)]
            MR = [sb(st, f"{tag}mr{i}", [128, 512], F32) for i in range(2)]
            banks = {}

            def stats(ti):
                t0, tl = tgs[ti]
                b1 = cx.bank(); b2 = cx.bank()
                banks[ti] = (b1, b2)
                for d in range(NKC):
                    s = d % 2
                    zk = f"z{d}_{ti}"
                    zap = Z[:, d, t0:t0 + tl]
                    cx.op(POOL, lambda: nc.gpsimd.tensor_tensor(out=SQ[s][:, 0:tl], in0=zap, in1=zap, op=ALU.mult),
                          reads=[zk], writes=[f"sq{s}"])
                    rd = [zk, f"sq{s}", "ones"]
                    wr = [b1.key, b2.key]
                    cx.deps(PE, rd, wr)
                    nc.tensor.matmul(b1.ap[:, 0:tl], lhsT=ONES[:], rhs=zap, start=(d == 0), stop=(d == NKC - 1))
                    m2 = nc.tensor.matmul(b2.ap[:, 0:tl], lhsT=ONES[:], rhs=SQ[s][:, 0:tl], start=(d == 0),
                                          stop=(d == NKC - 1))
                    m2.then_inc(PE.sem.h, 1)
                    PE.sem.count += 1
                    cx.trace.setdefault("pe", []).append(("i", id(PE.sem), 1))
                    cx.commit((PE.sem, PE.sem.count), rd, wr)

            def finish(ti):
                t0, tl = tgs[ti]
                b1, b2 = banks[ti]
                m = ti % 2
                mean, rstd, mr = MEAN[m][:, 0:tl], RSTD[m][:, 0:tl], MR[m][:, 0:tl]
                cx.op(ACT, lambda: nc.scalar.activation(out=mean, in_=b1.ap[:, 0:tl], func=AF.Identity, scale=1.0 / D),
                      reads=[b1.key], writes=[f"mean{m}"])
                cx.op(DVE, lambda: nc.vector.tensor_tensor(out=mr, in0=mean, in1=mean, op=ALU.mult),
                      reads=[f"mean{m}"], writes=[f"mr{m}"])
                cx.op(DVE, lambda: nc.vector.scalar_tensor_tensor(out=rstd, in0=b2.ap[:, 0:tl], scalar=1.0 / D, in1=mr,
                                                                  op0=ALU.mult, op1=ALU.subtract),
                      reads=[b2.key, f"mr{m}"], writes=[f"rstd{m}"])
                cx.op(ACT, lambda: nc.scalar.activation(out=rstd, in_=rstd, func=AF.Sqrt, bias=SM[:, C_EPS:C_EPS + 1],
                                                        scale=1.0), reads=[f"rstd{m}", "sm"], writes=[f"rstd{m}"])
                cx.op(DVE, lambda: nc.vector.reciprocal(out=rstd, in_=rstd), reads=[f"rstd{m}"], writes=[f"rstd{m}"])
                cx.op(DVE, lambda: nc.vector.tensor_tensor(out=mr, in0=mean, in1=rstd, op=ALU.mult),
                      reads=[f"mean{m}", f"rstd{m}"], writes=[f"mr{m}"])

            def norm(ti):
                t0, tl = tgs[ti]
                m = ti % 2
                rstd, mr = RSTD[m][:, 0:tl], MR[m][:, 0:tl]
                for d in range(NKC):
                    s = d % 2
                    zk = f"z{d}_{ti}"
                    zap = Z[:, d, t0:t0 + tl]
                    cx.op(DVE, lambda: nc.vector.tensor_tensor(out=TMP[s][:, 0:tl], in0=zap, in1=rstd, op=ALU.mult),
                          reads=[zk, f"rstd{m}"], writes=[f"tmp{s}"])
                    if d >= 1:
                        sub(ti, d - 1)
                sub(ti, NKC - 1)

            def sub(ti, d):
                t0, tl = tgs[ti]
                m = ti % 2
                mr = MR[m][:, 0:tl]
                s = d % 2
                zk = f"z{d}_{ti}"
                zap = Z[:, d, t0:t0 + tl]
                cx.op(DVE, lambda: nc.vector.tensor_tensor(out=zap, in0=TMP[s][:, 0:tl], in1=mr, op=ALU.subtract),
                      reads=[f"tmp{s}", f"mr{m}"], writes=[zk])
                if final:
                    cx.op(ACT, lambda: nc.scalar.activation(out=zap, in_=zap, func=AF.Identity,
                                                            bias=SM[:, bc + d:bc + d + 1], scale=SM[:, gc + d:gc + d + 1]),
                          reads=[zk, "sm"], writes=[zk])
                else:
                    cx.op(ACT, lambda: nc.scalar.activation(out=HB[:, d, t0:t0 + tl], in_=zap, func=AF.Identity,
                                                            bias=SM[:, bc + d:bc + d + 1], scale=SM[:, gc + d:gc + d + 1]),
                          reads=[zk, "sm"], writes=[f"hb{d}_{ti}"])

            n = len(tgs)
            stats(0)
            for i in range(n):
                finish(i)
                if i + 1 < n:
                    stats(i + 1)
                norm(i)
                if final:
                    t0, tl = tgs[i]
                    cx.dma(SQ_, osem, out=ov[:, :, t0 - HALO:t0 - HALO + tl], in_=Z[:, :, t0:t0 + tl],
                           reads=[f"z{d}_{i}" for d in range(NKC)])

    f1 = FFN("f1", w1g, w1u, w1d, TG_ALL, lambda ti: ["hbinit"])
    f1.prefetch()
    f1.run()

    p2 = ExitStack()
    W2 = [sb(p2, f"w2s{i}", [128, NKC * 384], BF16) for i in range(3)]
    MIX = sb(p2, "MIX", [128, 6, TOWN], BF16)
    MEMK = sb(p2, "MEMK", [128, 4, MEM_LEN], BF16)
    MEMV = sb(p2, "MEMV", [128, 2, 512], BF16)
    w_in_v = w_in.rearrange("(kc p) c -> p kc c", p=128)
    w_mkv_v = w_mkv.rearrange("(kc p) c -> p kc c", p=128)
    w_out_v = w_out.rearrange("(r p) d -> p r d", p=128)
    slot_i = [0]
    z_started = [0]

    def wslot():
        i = slot_i[0]
        slot_i[0] = (i + 1) % 3
        return i

    def load_cols(src_v, col_specs):
        i = wslot()
        w = sum(n for _, n in col_specs)
        view = W2[i][:, 0:NKC * w].rearrange("p (kc c) -> p kc c", kc=NKC)
        off = 0
        for (c0, n) in col_specs:
            cx.dma(PQ, f"w2_{i}", out=view[:, :, off:off + n], in_=src_v[:, :, c0:c0 + n], writes=[f"w2_{i}"])
            off += n
        return view, f"w2_{i}"

    def swa_cols(g):
        cols = [(2304 + 192 * g, 192), (3072 + 64 * g, 64)]
        if g % 2 == 0:
            cols.append((3328 + 64 * g, 128))
        return cols

    def load_wo(r0, nr):
        i = wslot()
        view = W2[i][:, 0:nr * D].rearrange("p (r d) -> p r d", r=nr)
        cx.dma(PQ, f"w2_{i}", out=view, in_=w_out_v[:, r0:r0 + nr, :], writes=[f"w2_{i}"])
        return view, f"w2_{i}"

    def out_proj(mix_chunk0, wo_r0, nr, wo_pre=None):
        view, key = wo_pre if wo_pre is not None else load_wo(wo_r0, nr)
        for d in range(NKC):
            for ti, (t0, tl) in enumerate(TG_OWN):
                b = cx.bank()
                cx.mm_group(b.ap[:, 0:tl],
                            [(view[:, i, d * 128:(d + 1) * 128], MIX[:, mix_chunk0 + i, t0 - HALO:t0 - HALO + tl])
                             for i in range(nr)],
                            reads=[key] + [f"mix{mix_chunk0 + i}_{ti}" for i in range(nr)], writes=[b.key])
                zk = f"z{d}_{ti}"
                zap = Z[:, d, t0:t0 + tl]
                if z_started[0] == 0:
                    cx.op(DVE, lambda: nc.vector.scalar_tensor_tensor(
                        out=zap, in0=zap, scalar=DER2[:, d:d + 1], in1=b.ap[:, 0:tl], op0=ALU.mult, op1=ALU.add),
                        reads=[b.key, zk], writes=[zk])
                elif z_started[0] == 1:
                    cx.op(DVE, lambda: nc.vector.scalar_tensor_tensor(
                        out=zap, in0=b.ap[:, 0:tl], scalar=DER2[:, 16 + d:17 + d], in1=zap, op0=ALU.add, op1=ALU.add),
                        reads=[b.key, zk], writes=[zk])
                else:
                    cx.op(DVE, lambda: nc.vector.tensor_tensor(out=zap, in0=b.ap[:, 0:tl], in1=zap, op=ALU.add),
                          reads=[b.key, zk], writes=[zk])
        z_started[0] += 1

    mkv_units = [(c0, nch, load_cols(w_mkv_v, [(c0, nch * 128)])) for (c0, nch) in [(0, 3), (384, 3), (768, 2)]]
    stM = ExitStack()
    MEMT = sb(stM, "MEMT", [128, NKC, MEM_LEN], BF16)
    cx.dma(PQ, "memt", out=MEMT[:], in_=memT.rearrange("(kc p) t -> p kc t", p=128), writes=["memt"])
    layernorm("l1", TG_ALL, 0, False)

    with stM:
        for (c0, nch, (view, key)) in mkv_units:
            for j in range(nch):
                col = c0 + j * 128
                if col < 512:
                    hh = col // 128
                    b = cx.bank()
                    cx.mm_group(b.ap[:, 0:MEM_LEN],
                                [(view[:, kc, j * 128:(j + 1) * 128], MEMT[:, kc, :]) for kc in range(NKC)],
                                reads=[key, "memt"], writes=[b.key])
                    cx.op(ACT, lambda: nc.scalar.copy(out=MEMK[:, hh, :], in_=b.ap[:, 0:MEM_LEN]),
                          reads=[b.key], writes=[f"memk{hh}"])
                else:
                    vc = (col - 512)
                    for c in range(2):
                        b = cx.bank()
                        cx.mm_group(b.ap[:, 0:128],
                                    [(MEMT[:, kc, c * 128:(c + 1) * 128], view[:, kc, j * 128:(j + 1) * 128])
                                     for kc in range(NKC)],
                                    reads=[key, "memt"], writes=[b.key])
                        cx.op(DVE, lambda: nc.vector.tensor_copy(out=MEMV[:, c, vc:vc + 128], in_=b.ap[:, 0:128]),
                              reads=[b.key], writes=[f"memv{c}_{vc}"])
        cx.barrier()

    with ExitStack() as st:
        CB = [sb(st, f"convC{i}", [128, TOWN + 2], F32) for i in range(2)]
        CU = [sb(st, f"convCU{i}", [128, TOWN + 2], F32) for i in range(2)]
        YB = [sb(st, f"convY{i}", [128, TOWN], F32) for i in range(2)]
        tg_cu = [(126, 2), (128, 512), (640, 512)]
        wo_pre = None
        for j in range(6):
            s = j % 2
            view, key = load_cols(w_in_v, [(j * 128, 128), (768 + j * 128, 128), (1536 + j * 128, 128)])
            if j == 4:
                wo_pre = load_wo(0, 3)
            for (t0, tl) in tg_cu:
                b = cx.bank()
                cx.mm_group(b.ap[:, 0:tl], [(view[:, kc, 128:256], HB[:, kc, t0:t0 + tl]) for kc in range(NKC)],
                            reads=[key], writes=[b.key])
                cx.op(ACT, lambda: nc.scalar.activation(out=CB[s][:, t0 - 126:t0 - 126 + tl], in_=b.ap[:, 0:tl],
                                                        func=AF.Identity,
                                                        bias=SM[:, C_BIN + 6 + j:C_BIN + 7 + j], scale=1.0),
                      reads=[b.key, "sm"], writes=[f"cb{s}_{t0}"])
            for (t0, tl) in tg_cu:
                b = cx.bank()
                cx.mm_group(b.ap[:, 0:tl], [(view[:, kc, 256:384], HB[:, kc, t0:t0 + tl]) for kc in range(NKC)],
                            reads=[key], writes=[b.key])
                cx.op(DVE, lambda: nc.vector.scalar_tensor_tensor(
                    out=CU[s][:, t0 - 126:t0 - 126 + tl], in0=b.ap[:, 0:tl],
                    scalar=SM[:, C_BIN + 12 + j:C_BIN + 13 + j], in1=CB[s][:, t0 - 126:t0 - 126 + tl],
                    op0=ALU.add, op1=ALU.mult), reads=[b.key, f"cb{s}_{t0}", "sm"], writes=[f"cu{s}_{t0}"])
            cx.op(DVE, lambda: nc.vector.tensor_scalar(out=CU[s][:, 0:2], in0=CU[s][:, 0:2],
                                                       scalar1=SM[:, C_FLAG:C_FLAG + 1], scalar2=None, op0=ALU.mult),
                  reads=["cu%d_126" % s, "sm"], writes=["cu%d_126" % s])
            cuk = [f"cu{s}_126", f"cu{s}_128", f"cu{s}_640"]
            cx.op(ACT, lambda: nc.scalar.activation(out=YB[s][:], in_=CU[s][:, 2:TOWN + 2], func=AF.Identity,
                                                    scale=SM[:, C_CW + 12 + j:C_CW + 13 + j]),
                  reads=cuk + ["sm"], writes=[f"y{s}"])
            cx.op(DVE, lambda: nc.vector.scalar_tensor_tensor(
                out=YB[s][:], in0=CU[s][:, 1:TOWN + 1], scalar=SM[:, C_CW + 6 + j:C_CW + 7 + j], in1=YB[s][:],
                op0=ALU.mult, op1=ALU.add), reads=cuk + [f"y{s}", "sm"], writes=[f"y{s}"])
            cx.op(DVE, lambda: nc.vector.scalar_tensor_tensor(
                out=YB[s][:], in0=CU[s][:, 0:TOWN], scalar=SM[:, C_CW + j:C_CW + 1 + j], in1=YB[s][:],
                op0=ALU.mult, op1=ALU.add), reads=cuk + [f"y{s}", "sm"], writes=[f"y{s}"])
            for ti, (t0, tl) in enumerate(TG_OWN):
                b = cx.bank()
                cx.mm_group(b.ap[:, 0:tl], [(view[:, kc, 0:128], HB[:, kc, t0:t0 + tl]) for kc in range(NKC)],
                            reads=[key], writes=[b.key])
                cx.op(DVE, lambda: nc.vector.scalar_tensor_tensor(
                    out=MIX[:, j, t0 - HALO:t0 - HALO + tl], in0=b.ap[:, 0:tl],
                    scalar=SM[:, C_BIN + j:C_BIN + 1 + j], in1=YB[s][:, t0 - HALO:t0 - HALO + tl],
                    op0=ALU.add, op1=ALU.mult), reads=[b.key, f"y{s}", "sm"], writes=[f"mix{j}_{ti}"])
        out_proj(0, 0, 3, wo_pre)
        wo2 = load_wo(3, 3)
        q_pre = load_cols(w_in_v, swa_cols(0))
        out_proj(3, 3, 3, wo2)
        cx.barrier()

    with ExitStack() as st:
        QT = sb(st, "QT", [64, 3, TOWN], BF16)
        KT = sb(st, "KT", [64, T], BF16)
        VA = sb(st, "VA", [128, 9, 2, 128], BF16)
        BIASF = sb(st, "BIASF", [128, 2, 3, 128], F32)
        BIAS0F = sb(st, "BIAS0F", [128, 3, 128], F32)
        BH = [sb(st, f"BH{i}", [128, 2, 3, 128], BF16) for i in range(2)]
        BL = [sb(st, f"BL{i}", [128, 2, 3, 128], BF16) for i in range(2)]
        B0H = [sb(st, f"B0H{i}", [128, 3, 128], BF16) for i in range(2)]
        B0L = [sb(st, f"B0L{i}", [128, 3, 128], BF16) for i in range(2)]
        ES2 = sb(st, "ES2", [64, 12, 128], BF16)
        TH = sb(st, "TH", [64, 12], BF16)
        TL = sb(st, "TL", [64, 12], F32)
        IDB = sb(st, "IDB", [128, 128], BF16)
        EB = [sb(st, f"EB{i}", [128, 2, 384], BF16) for i in range(2)]
        RD = [sb(st, f"RD{i}", [128, 384], F32) for i in range(2)]
        bias_v = biasT_d.rearrange("p (k h q) -> p k h q", k=2, h=12)
        bias0_v = bias0T_d.rearrange("p (h q) -> p h q", h=12)
        cx.dma(PQ, "idb", out=IDB[:], in_=ident_d, writes=["idb"])
        cx.op(DVE, lambda: nc.vector.tensor_copy(out=TH[:], in_=DER[0:64, 16:28]), writes=["th"])
        cx.op(DVE, lambda: nc.vector.tensor_tensor(out=TL[:], in0=DER[0:64, 16:28], in1=TH[:], op=ALU.subtract),
              reads=["th"], writes=["tl"])
        cx.op(DVE, lambda: nc.vector.tensor_scalar(out=ES2[0:32], in0=TH[0:32].unsqueeze(2).to_broadcast([32, 12, 128]),
                                                   scalar1=1.0 / 32, scalar2=None, op0=ALU.mult),
              reads=["th"], writes=["es2a"])
        cx.op(DVE, lambda: nc.vector.tensor_scalar(out=ES2[32:64], in0=TL[32:64].unsqueeze(2).to_broadcast([32, 12, 128]),
                                                   scalar1=1.0 / 32, scalar2=None, op0=ALU.mult),
              reads=["tl"], writes=["es2b"])
        wo_pre = None
        for g in range(4):
            sg, gi = g // 2, g % 2
            gb = g % 2
            view, key = q_pre if g == 0 else load_cols(w_in_v, swa_cols(g))
            if g == 3:
                wo_pre = load_wo(6, 3)
            cx.dma(SQ_, "biasf", out=BIASF[:], in_=bias_v[:, :, 3 * g:3 * g + 3, :], writes=["biasf"])
            cx.dma(SQ_, "bias0f", out=BIAS0F[:], in_=bias0_v[:, 3 * g:3 * g + 3, :], writes=["bias0f"])
            cx.op(ACT, lambda: nc.scalar.copy(out=BH[gb][:], in_=BIASF[:]), reads=["biasf"], writes=[f"bh{gb}"])
            cx.op(DVE, lambda: nc.vector.tensor_tensor(out=BL[gb][:], in0=BIASF[:], in1=BH[gb][:], op=ALU.subtract),
                  reads=["biasf", f"bh{gb}"], writes=[f"bl{gb}"])
            cx.op(ACT, lambda: nc.scalar.copy(out=B0H[gb][:], in_=BIAS0F[:]), reads=["bias0f"], writes=[f"b0h{gb}"])
            cx.op(DVE, lambda: nc.vector.tensor_tensor(out=B0L[gb][:], in0=BIAS0F[:], in1=B0H[gb][:], op=ALU.subtract),
                  reads=["bias0f", f"b0h{gb}"], writes=[f"b0l{gb}"])
            for jh in range(3):
                h = 3 * g + jh
                for ti, (t0, tl) in enumerate(TG_OWN):
                    b = cx.bank()
                    cx.mm_group(b.ap[0:64, 0:tl],
                                [(view[:, kc, jh * 64:(jh + 1) * 64], HB[:, kc, t0:t0 + tl]) for kc in range(NKC)],
                                reads=[key], writes=[b.key])
                    cx.op(ACT, lambda: nc.scalar.activation(out=QT[:, jh, t0 - HALO:t0 - HALO + tl],
                                                            in_=b.ap[0:64, 0:tl], func=AF.Identity,
                                                            bias=DER[0:64, h:h + 1], scale=0.125),
                          reads=[b.key, "der0"], writes=[f"qt{jh}_{ti}"])
            for (t0, tl) in TG_ALL:
                b = cx.bank()
                cx.mm_group(b.ap[0:64, 0:tl],
                            [(view[:, kc, 192:256], HB[:, kc, t0:t0 + tl]) for kc in range(NKC)],
                            reads=[key], writes=[b.key])
                cx.op(ACT, lambda: nc.scalar.activation(out=KT[:, t0:t0 + tl], in_=b.ap[0:64, 0:tl],
                                                        func=AF.Identity, bias=SM[0:64, C_BK + g:C_BK + g + 1],
                                                        scale=1.0), reads=[b.key, "sm"], writes=["kt"])
            if gi == 0:
                for blk in range(9):
                    b = cx.bank()
                    cx.mm_group(b.ap[:, 0:128],
                                [(HB[:, kc, blk * 128:(blk + 1) * 128], view[:, kc, 256:384]) for kc in range(NKC)],
                                reads=[key], writes=[b.key])
                    for hf in range(2):
                        cx.op(DVE, lambda: nc.vector.tensor_tensor(
                            out=VA[:, blk, :, hf * 64:(hf + 1) * 64],
                            in0=b.ap[:, 0:128].rearrange("p (g d) -> p g d", g=2),
                            in1=SM[:, C_BV + 128 * sg:C_BV + 128 * sg + 128].rearrange("p (g d) -> p g d", g=2),
                            op=ALU.add), reads=[b.key, "sm"], writes=["va"])

            def ST(n, c):
                pt, pk = cx.bank2()
                for kb in range(2):
                    if n == 0 and kb == 0:
                        hi, lo = B0H[gb][:].rearrange("p h q -> p (h q)"), B0L[gb][:].rearrange("p h q -> p (h q)")
                        bkeys = [f"b0h{gb}", f"b0l{gb}"]
                    else:
                        hi = BH[gb][:, kb].rearrange("p h q -> p (h q)")
                        lo = BL[gb][:, kb].rearrange("p h q -> p (h q)")
                        bkeys = [f"bh{gb}", f"bl{gb}"]
                    cx.mm_group(pt[:, kb, 0:384],
                                [(IDB[:], hi), (IDB[:], lo),
                                 (KT[:, (n + kb) * 128:(n + kb + 1) * 128], QT[:, :, n * 128:(n + 1) * 128])],
                                reads=bkeys + ["idb", "kt"] + [f"qt{jh}_{n // 4}" for jh in range(3)], writes=[pk[kb]])
                cx.op(ACT, lambda: nc.scalar.activation(out=EB[c][:], in_=pt[:, :, 0:384], func=AF.Exp),
                      reads=pk, writes=[f"eb{c}"])

            def PV(n, c):
                bn = cx.bank(); bd = cx.bank()
                cx.mm_group(bn.ap[:, 0:384], [(VA[:, n + kb, gi, :], EB[c][:, kb, :]) for kb in range(2)],
                            reads=["va", f"eb{c}"], writes=[bn.key])
                cx.mm_group(bd.ap[:, 0:384],
                            [(ONESB[:], EB[c][:, kb, :]) for kb in range(2)]
                            + [(ONESB[0:64, :], ES2[:, 3 * g:3 * g + 3, :])],
                            reads=["onesb", f"eb{c}", "es2a", "es2b"], writes=[bd.key])
                cx.op(DVE, lambda: nc.vector.reciprocal(out=RD[c][:], in_=bd.ap[:, 0:384]), reads=[bd.key],
                      writes=[f"rd{c}"])
                for jh in range(3):
                    h = 3 * g + jh
                    p0 = (h % 2) * 64
                    cx.op(DVE, lambda: nc.vector.tensor_tensor(
                        out=MIX[p0:p0 + 64, h // 2, n * 128:(n + 1) * 128],
                        in0=bn.ap[p0:p0 + 64, jh * 128:(jh + 1) * 128], in1=RD[c][p0:p0 + 64, jh * 128:(jh + 1) * 128],
                        op=ALU.mult), reads=[bn.key, f"rd{c}"], writes=[f"mix{h // 2}_{n // 4}"])

            ST(0, 0)
            for n in range(8):
                if n + 1 < 8:
                    ST(n + 1, (n + 1) % 2)
                PV(n, n % 2)
        out_proj(0, 6, 3, wo_pre)
        wo2 = load_wo(9, 3)
        qm_pre = load_cols(w_in_v, [(3584, 256)])
        out_proj(3, 9, 3, wo2)
        cx.barrier()

    with ExitStack() as st:
        QM = [sb(st, f"QM{i}", [128, TOWN], BF16) for i in range(2)]
        E2 = [sb(st, f"E2{i}", [128, 2, 512], BF16) for i in range(2)]
        RD2 = [sb(st, f"RD2{i}", [128, 512], F32) for i in range(2)]
        cnt = 0
        wo_pre = None
        for hp in range(2):
            view, key = qm_pre if hp == 0 else load_cols(w_in_v, [(3584 + 256 * hp, 256)])
            if hp == 1:
                wo_pre = load_wo(12, 2)
            for hh in range(2):
                h = 2 * hp + hh
                qs = h % 2
                for ti, (t0, tl) in enumerate(TG_OWN):
                    b = cx.bank()
                    cx.mm_group(b.ap[:, 0:tl],
                                [(view[:, kc, hh * 128:(hh + 1) * 128], HB[:, kc, t0:t0 + tl]) for kc in range(NKC)],
                                reads=[key], writes=[b.key])
                    cx.op(ACT, lambda: nc.scalar.activation(out=QM[qs][:, t0 - HALO:t0 - HALO + tl], in_=b.ap[:, 0:tl],
                                                            func=AF.Identity, bias=DER[:, 12 + h:13 + h],
                                                            scale=128.0 ** -0.5),
                          reads=[b.key, "der1"], writes=[f"qm{qs}_{ti}"])
                for ti, (t0, tl) in enumerate(TG_OWN):
                    c = cnt % 2
                    cnt += 1
                    pt, pk = cx.bank2()
                    for mc in range(2):
                        cx.mm_group(pt[:, mc, 0:tl],
                                    [(MEMK[:, h, mc * 128:(mc + 1) * 128], QM[qs][:, t0 - HALO:t0 - HALO + tl])],
                                    reads=[f"qm{qs}_{ti}"], writes=[pk[mc]])
                    cx.op(ACT, lambda: nc.scalar.activation(out=E2[c][:], in_=pt[:], func=AF.Exp),
                          reads=pk, writes=[f"e2{c}"])
                    bn = cx.bank(); bd = cx.bank()
                    cx.mm_group(bn.ap[:, 0:tl],
                                [(MEMV[:, mc, h * 128:(h + 1) * 128], E2[c][:, mc, :]) for mc in range(2)],
                                reads=[f"e2{c}"], writes=[bn.key])
                    cx.mm_group(bd.ap[:, 0:tl], [(ONESB[:], E2[c][:, mc, :]) for mc in range(2)],
                                reads=[f"e2{c}", "onesb"], writes=[bd.key])
                    cx.op(DVE, lambda: nc.vector.reciprocal(out=RD2[c][:], in_=bd.ap[:, 0:tl]), reads=[bd.key],
                          writes=[f"rd2{c}"])
                    cx.op(DVE, lambda: nc.vector.tensor_tensor(out=MIX[:, h, t0 - HALO:t0 - HALO + tl],
                                                               in0=bn.ap[:, 0:tl], in1=RD2[c][:], op=ALU.mult),
                          reads=[bn.key, f"rd2{c}"], writes=[f"mix{h}_{ti}"])
        out_proj(0, 12, 2, wo_pre)
        out_proj(2, 14, 2)
        cx.barrier()
    p2.close()

    f2 = FFN("f2", w2g, w2u, w2d, TG_OWN, lambda ti: [f"hb{d}_{ti}" for d in range(NKC)], ln_prev=1)
    f2.prefetch()
    layernorm("l2", TG_OWN, 1, False)
    f2.run()
    layernorm("l3", TG_OWN, 2, True)

    nc.sync.wait_ge(osem.h, osem.count)
    es.close()
    nc._cx_trace = cx.trace if hasattr(nc, "__dict__") else None
    build_program.last_trace = cx.trace
    return nc


def _t5_bucket(dist):
    max_exact = 16
    d = np.maximum(dist, 1).astype(np.float64)
    large = max_exact + (np.log(d / max_exact) / math.log(128 / max_exact) * (32 - max_exact)).astype(np.int32)
    large = np.minimum(large, 31)
    return np.where(dist < max_exact, dist, large)


_NC_CACHE = {}


def kernel(x, mem, ln1_g, ln1_b, ffn1_w_gate, ffn1_w_up, ffn1_w_down, w_in, b_in, conv_w, sinks, w_mem_kv,
           w_out, ln2_g, ln2_b, ffn2_w_gate, ffn2_w_up, ffn2_w_down, ln3_g, ln3_b, rel_bias):
    f32 = np.float32
    x = np.asarray(x, f32); mem = np.asarray(mem, f32)
    b_in0 = np.asarray(b_in, f32)[0]
    rel_bias = np.asarray(rel_bias, f32)

    sm = np.zeros((128, NS), f32)
    for i, v in enumerate([ln1_g, ln1_b, ln2_g, ln2_b, ln3_g, ln3_b]):
        sm[:, C_LN + 16 * i:C_LN + 16 * (i + 1)] = np.asarray(v, f32)[0].reshape(16, 128).T
    sm[:, C_BIN:C_BIN + 32] = b_in0.reshape(32, 128).T
    sm[0:64, C_BQ:C_BQ + 12] = b_in0[2304:3072].reshape(12, 64).T
    sm[:, C_BK:C_BK + 4] = np.tile(b_in0[3072:3328].reshape(4, 64).T, (2, 1))
    cw = np.asarray(conv_w, f32)[0]
    for tap in range(3):
        sm[:, C_CW + 6 * tap:C_CW + 6 * (tap + 1)] = cw[tap].reshape(6, 128).T
    sm[:, C_SINK:C_SINK + 12] = np.asarray(sinks, f32)[0][None, :]
    sm[:, C_EPS] = LN_EPS
    sm[:, C_BV:C_BV + 256] = b_in0[3328:3584][None, :]

    kk = np.arange(128)[:, None]
    qq = np.arange(128)[None, :]
    bias_t = np.full((128, 2, 12, 128), NEG, f32)
    for kb in range(2):
        dist = qq + 128 - (kk + 128 * kb)
        ok = (dist >= 0) & (dist < 128)
        bk = _t5_bucket(np.maximum(dist, 0))
        vals = rel_bias[bk]
        bias_t[:, kb] = np.where(ok[:, None, :], vals.transpose(0, 2, 1), f32(NEG))
    bias_t_flat = np.ascontiguousarray(bias_t.reshape(128, -1))
    bias0_valid = np.ascontiguousarray(bias_t[:, 0].reshape(128, -1))
    bias0_masked = np.full_like(bias0_valid, NEG)

    shared = {
        "w1g": np.ascontiguousarray(np.asarray(ffn1_w_gate, f32)[0]),
        "w1u": np.ascontiguousarray(np.asarray(ffn1_w_up, f32)[0]),
        "w1d": np.ascontiguousarray(np.asarray(ffn1_w_down, f32)[0]),
        "w2g": np.ascontiguousarray(np.asarray(ffn2_w_gate, f32)[0]),
        "w2u": np.ascontiguousarray(np.asarray(ffn2_w_up, f32)[0]),
        "w2d": np.ascontiguousarray(np.asarray(ffn2_w_down, f32)[0]),
        "w_in": np.ascontiguousarray(np.asarray(w_in, f32)[0]),
        "w_mkv": np.ascontiguousarray(np.asarray(w_mem_kv, f32)[0]),
        "w_out": np.ascontiguousarray(np.asarray(w_out, f32)[0]),
        "biasT": bias_t_flat,
        "ident": np.eye(128, dtype=f32),
    }
    in_maps = []
    for c in range(8):
        b, half = c // 2, c % 2
        own = x[b, half * TOWN:(half + 1) * TOWN]
        halo = x[b, TOWN - HALO:TOWN] if half == 1 else np.zeros((HALO, D), f32)
        xt = np.ascontiguousarray(np.concatenate([halo, own], axis=0).T)
        smc = sm.copy()
        smc[:, C_FLAG] = float(half)
        m = dict(shared)
        m["xT"] = xt
        m["memT"] = np.ascontiguousarray(mem[b].T)
        m["smalls"] = smc
        m["bias0T"] = bias0_valid if half == 1 else bias0_masked
        in_maps.append(m)

    if "nc" not in _NC_CACHE:
        _NC_CACHE["nc"] = build_program()
    nc = _NC_CACHE["nc"]
    res = run_bass_kernel_spmd(nc, in_maps, core_ids=list(range(8)))
    out = np.empty((BATCH, SEQ, D), f32)
    for c in range(8):
        b, half = c // 2, c % 2
        out[b, half * TOWN:(half + 1) * TOWN] = np.asarray(res.results[c]["outT"], f32).T
    return out
```

```python
import math
from contextlib import ExitStack

import numpy as np
import concourse.bass as bass
import concourse.mybir as mybir
from concourse.bass_utils import run_bass_kernel_spmd

F32 = mybir.dt.float32
BF16 = mybir.dt.bfloat16
AF = mybir.ActivationFunctionType
ALU = mybir.AluOpType

D = 2048
DFF = 5632
NKC = 16
NPART = DFF // 256
TOWN = 1024
HALO = 128
T = TOWN + HALO
SEQ = 2048
BATCH = 4
MEM_LEN = 256
ALPHA = 2.0 ** 0.25
LN_EPS = 1e-5
NEG = -30000.0
TG_OWN = [(128, 512), (640, 512)]
TG_ALL = [(0, 384), (384, 384), (768, 384)]
NS = 432
C_LN = 0
C_BIN = 96
C_BQ = 128
C_BK = 140
C_CW = 144
C_SINK = 162
C_FLAG = 174
C_EPS = 175
C_BV = 176


class Sem:
    def __init__(self, h):
        self.h = h
        self.count = 0


class Eng:
    def __init__(self, name, h, sem=None):
        self.name = name
        self.h = h
        self.sem = sem
        self.seen = {}


class Bank:
    def __init__(self, ap, key):
        self.ap = ap
        self.key = key


class Ctx:
    def __init__(self, nc, es):
        self.nc = nc
        self.es = es
        self.regions = {}
        self.sems = []
        self.bank_i = 0
        self.trace = {}
        self.dsems = {}

    def new_sem(self, name):
        s = Sem(self.es.enter_context(self.nc.semaphore(name)))
        self.sems.append(s)
        return s

    def _wait(self, eng, tok):
        sem, val = tok
        if eng.seen.get(sem, 0) >= val:
            return
        eng.h.wait_ge(sem.h, val)
        eng.seen[sem] = val
        self.trace.setdefault(eng.name, []).append(("w", id(sem), val))

    def deps(self, eng, reads, writes, own=None):
        own = own if own is not None else eng.sem
        for k in reads:
            r = self.regions.get(k)
            if r is not None and r[0] is not None:
                self._wait(eng, r[0])
        for k in writes:
            r = self.regions.get(k)
            if r is not None:
                if r[0] is not None and r[0][0] is not own:
                    self._wait(eng, r[0])
                for s, v in r[1].items():
                    if s is not own:
                        self._wait(eng, (s, v))

    def commit(self, tok, reads, writes):
        for k in writes:
            self.regions[k] = [tok, {}]
        for k in reads:
            r = self.regions.setdefault(k, [None, {}])
            if r[1].get(tok[0], 0) < tok[1]:
                r[1][tok[0]] = tok[1]

    def op(self, eng, fn, reads=(), writes=()):
        self.deps(eng, reads, writes)
        ins = fn()
        ins.then_inc(eng.sem.h, 1)
        eng.sem.count += 1
        self.trace.setdefault(eng.name, []).append(("i", id(eng.sem), 1))
        self.commit((eng.sem, eng.sem.count), reads, writes)

    def mm_group(self, out, pairs, reads, writes):
        pe = self.PE
        self.deps(pe, reads, writes)
        n = len(pairs)
        ins = None
        for i, (l, r) in enumerate(pairs):
            ins = self.nc.tensor.matmul(out, lhsT=l, rhs=r, start=(i == 0), stop=(i == n - 1))
        ins.then_inc(pe.sem.h, 1)
        pe.sem.count += 1
        self.trace.setdefault(pe.name, []).append(("i", id(pe.sem), 1))
        self.commit((pe.sem, pe.sem.count), reads, writes)

    def dma(self, q, dsem, out, in_, reads=(), writes=()):
        if isinstance(dsem, str):
            if dsem not in self.dsems:
                self.dsems[dsem] = self.new_sem("d_" + dsem)
            dsem = self.dsems[dsem]
        self.deps(q, reads, writes, own=dsem)
        q.h.dma_start(out=out, in_=in_).then_inc(dsem.h, 16)
        dsem.count += 16
        self.trace.setdefault(q.name, []).append(("i", id(dsem), 16))
        self.commit((dsem, dsem.count), reads, writes)

    def barrier(self):
        for e in self.engines:
            for s in self.sems:
                if s.count > 0:
                    self._wait(e, (s, s.count))
        self.regions = {}

    def bank(self):
        i = self.bank_i
        self.bank_i = (i + 1) % 8
        return Bank(self.PS[i // 2][:, i % 2, :], f"ps{i}")

    def bank2(self):
        if self.bank_i % 2:
            self.bank_i = (self.bank_i + 1) % 8
        i = self.bank_i
        self.bank_i = (i + 2) % 8
        return self.PS[i // 2], [f"ps{i}", f"ps{i + 1}"]


def build_program():
    nc = bass.Bass("TRN2", target_bir_lowering=False)
    es = ExitStack()
    cx = Ctx(nc, es)

    def din(name, shape):
        return nc.dram_tensor(name, shape, F32, kind="ExternalInput").ap()

    xT = din("xT", [D, T])
    memT = din("memT", [D, MEM_LEN])
    smalls_d = din("smalls", [128, NS])
    biasT_d = din("biasT", [128, 2 * 12 * 128])
    bias0T_d = din("bias0T", [128, 12 * 128])
    ident_d = din("ident", [128, 128])
    w1g = din("w1g", [D, DFF]); w1u = din("w1u", [D, DFF]); w1d = din("w1d", [DFF, D])
    w2g = din("w2g", [D, DFF]); w2u = din("w2u", [D, DFF]); w2d = din("w2d", [DFF, D])
    w_in = din("w_in", [D, 4096])
    w_mkv = din("w_mkv", [D, 1024])
    w_out = din("w_out", [D, D])
    outT = nc.dram_tensor("outT", [D, TOWN], F32, kind="ExternalOutput").ap()

    def sb(stack, name, shape, dt=F32):
        return stack.enter_context(nc.sbuf_tensor(name, shape, dt))

    Z = sb(es, "Z", [128, NKC, T], F32)
    HB = sb(es, "HB", [128, NKC, T], BF16)
    SM = sb(es, "SM", [128, NS], F32)
    DER = sb(es, "DER", [128, 32], F32)
    DER2 = sb(es, "DER2", [128, 64], F32)
    ONES = sb(es, "ONES", [128, 128], F32)
    ONESB = sb(es, "ONESB", [128, 128], BF16)
    cx.PS = [es.enter_context(nc.psum_tensor(f"ps{i}", [128, 2, 512], F32)) for i in range(4)]

    pe_sem = cx.new_sem("s_pe"); act_sem = cx.new_sem("s_act"); dve_sem = cx.new_sem("s_dve")
    pool_sem = cx.new_sem("s_pool")
    osem = cx.new_sem("s_o")
    PE = Eng("pe", nc.tensor, pe_sem); ACT = Eng("act", nc.scalar, act_sem); DVE = Eng("dve", nc.vector, dve_sem)
    PQ = Eng("pool", nc.gpsimd, pool_sem); SQ_ = Eng("spq", nc.sync, None)
    POOL = PQ
    cx.PE = PE
    cx.engines = [PE, ACT, DVE, PQ, SQ_]

    ov = outT.rearrange("(kc p) t -> p kc t", p=128)

    cx.dma(SQ_, "sm", out=SM[:], in_=smalls_d, writes=["sm"])
    xv = xT.rearrange("(kc p) t -> p kc t", p=128)
    for h in range(2):
        cx.dma(SQ_, f"zinit{h}", out=Z[:, h * 8:(h + 1) * 8, :], in_=xv[:, h * 8:(h + 1) * 8, :],
               writes=[f"z{d}_{ti}" for d in range(h * 8, (h + 1) * 8) for ti in range(3)])

    def load_hb(ti):
        t0, tl = TG_ALL[ti]
        cx.dma(PQ, f"hbinit{ti}", out=HB[:, :, t0:t0 + tl], in_=xv[:, :, t0:t0 + tl], writes=[f"hbinit{ti}"])
    cx.op(DVE, lambda: nc.vector.memset(ONES[:], 1.0), writes=["ones"])
    cx.op(DVE, lambda: nc.vector.memset(ONESB[:], 1.0), writes=["onesb"])
    cx.op(DVE, lambda: nc.vector.tensor_scalar(out=DER[:, 0:6], in0=SM[:, C_BIN + 18:C_BIN + 24], scalar1=0.125,
                                               scalar2=None, op0=ALU.mult), reads=["sm"], writes=["der0"])
    cx.op(DVE, lambda: nc.vector.tensor_scalar(out=DER2[:], in0=SM[:, 0:64], scalar1=ALPHA, scalar2=None,
                                               op0=ALU.mult), reads=["sm"], writes=["der3"])
    cx.op(DVE, lambda: nc.vector.tensor_scalar(out=DER[:, 12:16], in0=SM[:, C_BIN + 28:C_BIN + 32],
                                               scalar1=128.0 ** -0.5, scalar2=None, op0=ALU.mult),
          reads=["sm"], writes=["der1"])
    cx.op(ACT, lambda: nc.scalar.activation(out=DER[:, 16:28], in_=SM[:, C_SINK:C_SINK + 12], func=AF.Exp),
          reads=["sm"], writes=["der2"])

    class FFN:
        def __init__(self, tag, wg, wu, wd, tgs, hb_keys, ln_prev=None):
            self.ln_prev = ln_prev
            self.st = ExitStack()
            st = self.st
            self.tgs = tgs
            self.hb_keys = hb_keys
            self.WGU = [sb(st, f"{tag}wgu{i}", [128, 2, NKC, 256], BF16) for i in range(2)]
            self.WD = [sb(st, f"{tag}wd{i}", [128, 2, D], BF16) for i in range(2)]
            self.ACTB = [sb(st, f"{tag}act{i}", [128, 2, T], BF16) for i in range(2)]
            self.SCR = [sb(st, f"{tag}scr{i}", [128, 512], F32) for i in range(2)]
            self.wgv = wg.rearrange("(kc p) f -> p kc f", p=128)
            self.wuv = wu.rearrange("(kc p) f -> p kc f", p=128)
            self.wdv = wd.rearrange("(fc p) d -> p fc d", p=128)
            self.scr_i = 0

        def load_gu(self, p):
            s = p % 2
            cx.dma(PQ, f"wg{s}", out=self.WGU[s][:, 0], in_=self.wgv[:, :, p * 256:(p + 1) * 256], writes=[f"wg{s}"])
            cx.dma(PQ, f"wu{s}", out=self.WGU[s][:, 1], in_=self.wuv[:, :, p * 256:(p + 1) * 256], writes=[f"wu{s}"])

        def load_d(self, p):
            s = p % 2
            cx.dma(PQ, f"wd{s}", out=self.WD[s][:], in_=self.wdv[:, 2 * p:2 * p + 2, :], writes=[f"wd{s}"])

        def prefetch(self):
            self.load_gu(0); self.load_d(0); self.load_gu(1); self.load_d(1)

        def A(self, p):
            s = p % 2
            WGU, ACTB, SCR = self.WGU, self.ACTB, self.SCR
            for fi in range(2):
                for ti, (t0, tl) in enumerate(self.tgs):
                    bg = cx.bank(); bu = cx.bank()
                    cx.mm_group(bg.ap[:, 0:tl],
                                [(WGU[s][:, 0, kc, fi * 128:(fi + 1) * 128], HB[:, kc, t0:t0 + tl]) for kc in range(NKC)],
                                reads=[f"wg{s}"] + self.hb_keys(ti), writes=[bg.key])
                    cx.mm_group(bu.ap[:, 0:tl],
                                [(WGU[s][:, 1, kc, fi * 128:(fi + 1) * 128], HB[:, kc, t0:t0 + tl]) for kc in range(NKC)],
                                reads=[f"wu{s}"] + self.hb_keys(ti), writes=[bu.key])
                    c = self.scr_i; self.scr_i = 1 - c
                    cx.op(ACT, lambda: nc.scalar.activation(out=SCR[c][:, 0:tl], in_=bg.ap[:, 0:tl], func=AF.Silu),
                          reads=[bg.key], writes=[f"scr{c}"])
                    cx.op(DVE, lambda: nc.vector.scalar_tensor_tensor(
                        out=ACTB[s][:, fi, t0:t0 + tl], in0=bu.ap[:, 0:tl], scalar=0.5, in1=SCR[c][:, 0:tl],
                        op0=ALU.mult, op1=ALU.mult), reads=[bu.key, f"scr{c}"], writes=[f"act{s}_{fi}_{ti}"])

        def B(self, p):
            s = p % 2
            WD, ACTB = self.WD, self.ACTB
            for d in range(NKC):
                for ti, (t0, tl) in enumerate(self.tgs):
                    b = cx.bank()
                    cx.mm_group(b.ap[:, 0:tl],
                                [(WD[s][:, fi, d * 128:(d + 1) * 128], ACTB[s][:, fi, t0:t0 + tl]) for fi in range(2)],
                                reads=[f"wd{s}", f"act{s}_0_{ti}", f"act{s}_1_{ti}"], writes=[b.key])
                    zk = f"z{d}_{ti}"
                    zap = Z[:, d, t0:t0 + tl]
                    lp = self.ln_prev
                    if p == 0:
                        sc = ALPHA if lp is None else DER2[:, 32 * lp + d:32 * lp + d + 1]
                        cx.op(DVE, lambda: nc.vector.scalar_tensor_tensor(
                            out=zap, in0=zap, scalar=sc, in1=b.ap[:, 0:tl], op0=ALU.mult, op1=ALU.add),
                            reads=[b.key, zk, "der3"], writes=[zk])
                    elif p == 1 and lp is not None:
                        cx.op(DVE, lambda: nc.vector.scalar_tensor_tensor(
                            out=zap, in0=b.ap[:, 0:tl], scalar=DER2[:, 32 * lp + 16 + d:32 * lp + 17 + d], in1=zap,
                            op0=ALU.add, op1=ALU.add), reads=[b.key, zk, "der3"], writes=[zk])
                    else:
                        cx.op(DVE, lambda: nc.vector.tensor_tensor(out=zap, in0=b.ap[:, 0:tl], in1=zap, op=ALU.add),
                              reads=[b.key, zk], writes=[zk])

        def run(self):
            self.A(0)
            for p in range(NPART):
                if p + 1 < NPART:
                    self.A(p + 1)
                if p + 2 < NPART:
                    self.load_gu(p + 2)
                self.B(p)
                if p + 2 < NPART:
                    self.load_d(p + 2)
            cx.barrier()
            self.st.close()

    def layernorm(tag, tgs, ln_idx, final):
        gc = C_LN + ln_idx * 32
        bc = gc + 16
        with ExitStack() as st:
            SQ = [sb(st, f"{tag}sq{i}", [128, 512], F32) for i in range(2)]
            TMP = [sb(st, f"{tag}tmp{i}", [128, 512], F32) for i in range(2)]
            MEAN = [sb(st, f"{tag}mean{i}", [128, 512], F32) for i in range(2)]
            RSTD = [sb(st, f"{tag}rstd{i}", [128, 512], F32) for i in range(2)]
            MR = [sb(st, f"{tag}mr{i}", [128, 512], F32) for i in range(2)]
            banks = {}

            def stats(ti):
                t0, tl = tgs[ti]
                b1 = cx.bank(); b2 = cx.bank()
                banks[ti] = (b1, b2)
                for d in range(NKC):
                    s = d % 2
                    zk = f"z{d}_{ti}"
                    zap = Z[:, d, t0:t0 + tl]
                    cx.op(POOL, lambda: nc.gpsimd.tensor_tensor(out=SQ[s][:, 0:tl], in0=zap, in1=zap, op=ALU.mult),
                          reads=[zk], writes=[f"sq{s}"])
                    rd = [zk, f"sq{s}", "ones"]
                    wr = [b1.key, b2.key]
                    cx.deps(PE, rd, wr)
                    nc.tensor.matmul(b1.ap[:, 0:tl], lhsT=ONES[:], rhs=zap, start=(d == 0), stop=(d == NKC - 1))
                    m2 = nc.tensor.matmul(b2.ap[:, 0:tl], lhsT=ONES[:], rhs=SQ[s][:, 0:tl], start=(d == 0),
                                          stop=(d == NKC - 1))
                    m2.then_inc(PE.sem.h, 1)
                    PE.sem.count += 1
                    cx.trace.setdefault("pe", []).append(("i", id(PE.sem), 1))
                    cx.commit((PE.sem, PE.sem.count), rd, wr)

            def finish(ti):
                t0, tl = tgs[ti]
                b1, b2 = banks[ti]
                m = ti % 2
                mean, rstd, mr = MEAN[m][:, 0:tl], RSTD[m][:, 0:tl], MR[m][:, 0:tl]
                cx.op(ACT, lambda: nc.scalar.activation(out=mean, in_=b1.ap[:, 0:tl], func=AF.Identity, scale=1.0 / D),
                      reads=[b1.key], writes=[f"mean{m}"])
                cx.op(DVE, lambda: nc.vector.tensor_tensor(out=mr, in0=mean, in1=mean, op=ALU.mult),
                      reads=[f"mean{m}"], writes=[f"mr{m}"])
                cx.op(DVE, lambda: nc.vector.scalar_tensor_tensor(out=rstd, in0=b2.ap[:, 0:tl], scalar=1.0 / D, in1=mr,
                                                                  op0=ALU.mult, op1=ALU.subtract),
                      reads=[b2.key, f"mr{m}"], writes=[f"rstd{m}"])
                cx.op(ACT, lambda: nc.scalar.activation(out=rstd, in_=rstd, func=AF.Sqrt, bias=SM[:, C_EPS:C_EPS + 1],
                                                        scale=1.0), reads=[f"rstd{m}", "sm"], writes=[f"rstd{m}"])
                cx.op(DVE, lambda: nc.vector.reciprocal(out=rstd, in_=rstd), reads=[f"rstd{m}"], writes=[f"rstd{m}"])
                cx.op(DVE, lambda: nc.vector.tensor_tensor(out=mr, in0=mean, in1=rstd, op=ALU.mult),
                      reads=[f"mean{m}", f"rstd{m}"], writes=[f"mr{m}"])

            def norm(ti):
                t0, tl = tgs[ti]
                m = ti % 2
                rstd, mr = RSTD[m][:, 0:tl], MR[m][:, 0:tl]
                for d in range(NKC):
                    s = d % 2
                    zk = f"z{d}_{ti}"
                    zap = Z[:, d, t0:t0 + tl]
                    cx.op(DVE, lambda: nc.vector.tensor_tensor(out=TMP[s][:, 0:tl], in0=zap, in1=rstd, op=ALU.mult),
                          reads=[zk, f"rstd{m}"], writes=[f"tmp{s}"])
                    if d >= 1:
                        sub(ti, d - 1)
                sub(ti, NKC - 1)

            def sub(ti, d):
                t0, tl = tgs[ti]
                m = ti % 2
                mr = MR[m][:, 0:tl]
                s = d % 2
                zk = f"z{d}_{ti}"
                zap = Z[:, d, t0:t0 + tl]
                cx.op(DVE, lambda: nc.vector.tensor_tensor(out=zap, in0=TMP[s][:, 0:tl], in1=mr, op=ALU.subtract),
                      reads=[f"tmp{s}", f"mr{m}"], writes=[zk])
                if final:
                    cx.op(ACT, lambda: nc.scalar.activation(out=zap, in_=zap, func=AF.Identity,
                                                            bias=SM[:, bc + d:bc + d + 1], scale=SM[:, gc + d:gc + d + 1]),
                          reads=[zk, "sm"], writes=[zk])
                else:
                    cx.op(ACT, lambda: nc.scalar.activation(out=HB[:, d, t0:t0 + tl], in_=zap, func=AF.Identity,
                                                            bias=SM[:, bc + d:bc + d + 1], scale=SM[:, gc + d:gc + d + 1]),
                          reads=[zk, "sm"], writes=[f"hb{d}_{ti}"])

            n = len(tgs)
            stats(0)
            for i in range(n):
                finish(i)
                if i + 1 < n:
                    stats(i + 1)
                norm(i)
                if final:
                    t0, tl = tgs[i]
                    cx.dma(SQ_, osem, out=ov[:, :, t0 - HALO:t0 - HALO + tl], in_=Z[:, :, t0:t0 + tl],
                           reads=[f"z{d}_{i}" for d in range(NKC)])

    f1 = FFN("f1", w1g, w1u, w1d, TG_ALL, lambda ti: [f"hbinit{ti}"])
    load_hb(0); f1.load_gu(0); load_hb(1); load_hb(2); f1.load_d(0); f1.load_gu(1); f1.load_d(1)
    f1.run()

    p2 = ExitStack()
    W2 = [sb(p2, f"w2s{i}", [128, NKC * 384], BF16) for i in range(3)]
    MIX = sb(p2, "MIX", [128, 6, TOWN], BF16)
    MEMK = sb(p2, "MEMK", [128, 4, MEM_LEN], BF16)
    MEMV = sb(p2, "MEMV", [128, 2, 512], BF16)
    w_in_v = w_in.rearrange("(kc p) c -> p kc c", p=128)
    w_mkv_v = w_mkv.rearrange("(kc p) c -> p kc c", p=128)
    w_out_v = w_out.rearrange("(r p) d -> p r d", p=128)
    slot_i = [0]
    z_started = [0]

    def wslot():
        i = slot_i[0]
        slot_i[0] = (i + 1) % 3
        return i

    def load_cols(src_v, col_specs):
        i = wslot()
        w = sum(n for _, n in col_specs)
        view = W2[i][:, 0:NKC * w].rearrange("p (kc c) -> p kc c", kc=NKC)
        off = 0
        for (c0, n) in col_specs:
            cx.dma(PQ, f"w2_{i}", out=view[:, :, off:off + n], in_=src_v[:, :, c0:c0 + n], writes=[f"w2_{i}"])
            off += n
        return view, f"w2_{i}"

    def swa_cols(g):
        cols = [(2304 + 192 * g, 192), (3072 + 64 * g, 64)]
        if g % 2 == 0:
            cols.append((3328 + 64 * g, 128))
        return cols

    def load_wo(r0, nr):
        i = wslot()
        view = W2[i][:, 0:nr * D].rearrange("p (r d) -> p r d", r=nr)
        cx.dma(PQ, f"w2_{i}", out=view, in_=w_out_v[:, r0:r0 + nr, :], writes=[f"w2_{i}"])
        return view, f"w2_{i}"

    def out_proj(mix_chunk0, wo_r0, nr, wo_pre=None):
        view, key = wo_pre if wo_pre is not None else load_wo(wo_r0, nr)
        for d in range(NKC):
            for ti, (t0, tl) in enumerate(TG_OWN):
                b = cx.bank()
                cx.mm_group(b.ap[:, 0:tl],
                            [(view[:, i, d * 128:(d + 1) * 128], MIX[:, mix_chunk0 + i, t0 - HALO:t0 - HALO + tl])
                             for i in range(nr)],
                            reads=[key] + [f"mix{mix_chunk0 + i}_{ti}" for i in range(nr)], writes=[b.key])
                zk = f"z{d}_{ti}"
                zap = Z[:, d, t0:t0 + tl]
                if z_started[0] == 0:
                    cx.op(DVE, lambda: nc.vector.scalar_tensor_tensor(
                        out=zap, in0=zap, scalar=DER2[:, d:d + 1], in1=b.ap[:, 0:tl], op0=ALU.mult, op1=ALU.add),
                        reads=[b.key, zk], writes=[zk])
                elif z_started[0] == 1:
                    cx.op(DVE, lambda: nc.vector.scalar_tensor_tensor(
                        out=zap, in0=b.ap[:, 0:tl], scalar=DER2[:, 16 + d:17 + d], in1=zap, op0=ALU.add, op1=ALU.add),
                        reads=[b.key, zk], writes=[zk])
                else:
                    cx.op(DVE, lambda: nc.vector.tensor_tensor(out=zap, in0=b.ap[:, 0:tl], in1=zap, op=ALU.add),
                          reads=[b.key, zk], writes=[zk])
        z_started[0] += 1

    mkv_units = [(c0, nch, load_cols(w_mkv_v, [(c0, nch * 128)])) for (c0, nch) in [(0, 3), (384, 3), (768, 2)]]
    stM = ExitStack()
    MEMT = sb(stM, "MEMT", [128, NKC, MEM_LEN], BF16)
    cx.dma(PQ, "memt", out=MEMT[:], in_=memT.rearrange("(kc p) t -> p kc t", p=128), writes=["memt"])
    layernorm("l1", TG_ALL, 0, False)

    with stM:
        for (c0, nch, (view, key)) in mkv_units:
            for j in range(nch):
                col = c0 + j * 128
                if col < 512:
                    hh = col // 128
                    b = cx.bank()
                    cx.mm_group(b.ap[:, 0:MEM_LEN],
                                [(view[:, kc, j * 128:(j + 1) * 128], MEMT[:, kc, :]) for kc in range(NKC)],
                                reads=[key, "memt"], writes=[b.key])
                    cx.op(ACT, lambda: nc.scalar.copy(out=MEMK[:, hh, :], in_=b.ap[:, 0:MEM_LEN]),
                          reads=[b.key], writes=[f"memk{hh}"])
                else:
                    vc = (col - 512)
                    for c in range(2):
                        b = cx.bank()
                        cx.mm_group(b.ap[:, 0:128],
                                    [(MEMT[:, kc, c * 128:(c + 1) * 128], view[:, kc, j * 128:(j + 1) * 128])
                                     for kc in range(NKC)],
                                    reads=[key, "memt"], writes=[b.key])
                        cx.op(DVE, lambda: nc.vector.tensor_copy(out=MEMV[:, c, vc:vc + 128], in_=b.ap[:, 0:128]),
                              reads=[b.key], writes=[f"memv{c}_{vc}"])
        cx.barrier()

    with ExitStack() as st:
        CB = [sb(st, f"convC{i}", [128, TOWN + 2], F32) for i in range(2)]
        CU = [sb(st, f"convCU{i}", [128, TOWN + 2], F32) for i in range(2)]
        YB = [sb(st, f"convY{i}", [128, TOWN], F32) for i in range(2)]
        tg_cu = [(126, 2), (128, 512), (640, 512)]
        wo_pre = None
        for j in range(6):
            s = j % 2
            view, key = load_cols(w_in_v, [(j * 128, 128), (768 + j * 128, 128), (1536 + j * 128, 128)])
            if j == 4:
                wo_pre = load_wo(0, 3)
            for (t0, tl) in tg_cu:
                b = cx.bank()
                cx.mm_group(b.ap[:, 0:tl], [(view[:, kc, 128:256], HB[:, kc, t0:t0 + tl]) for kc in range(NKC)],
                            reads=[key], writes=[b.key])
                cx.op(ACT, lambda: nc.scalar.activation(out=CB[s][:, t0 - 126:t0 - 126 + tl], in_=b.ap[:, 0:tl],
                                                        func=AF.Identity,
                                                        bias=SM[:, C_BIN + 6 + j:C_BIN + 7 + j], scale=1.0),
                      reads=[b.key, "sm"], writes=[f"cb{s}_{t0}"])
            for (t0, tl) in tg_cu:
                b = cx.bank()
                cx.mm_group(b.ap[:, 0:tl], [(view[:, kc, 256:384], HB[:, kc, t0:t0 + tl]) for kc in range(NKC)],
                            reads=[key], writes=[b.key])
                cx.op(DVE, lambda: nc.vector.scalar_tensor_tensor(
                    out=CU[s][:, t0 - 126:t0 - 126 + tl], in0=b.ap[:, 0:tl],
                    scalar=SM[:, C_BIN + 12 + j:C_BIN + 13 + j], in1=CB[s][:, t0 - 126:t0 - 126 + tl],
                    op0=ALU.add, op1=ALU.mult), reads=[b.key, f"cb{s}_{t0}", "sm"], writes=[f"cu{s}_{t0}"])
            cx.op(DVE, lambda: nc.vector.tensor_scalar(out=CU[s][:, 0:2], in0=CU[s][:, 0:2],
                                                       scalar1=SM[:, C_FLAG:C_FLAG + 1], scalar2=None, op0=ALU.mult),
                  reads=["cu%d_126" % s, "sm"], writes=["cu%d_126" % s])
            cuk = [f"cu{s}_126", f"cu{s}_128", f"cu{s}_640"]
            cx.op(ACT, lambda: nc.scalar.activation(out=YB[s][:], in_=CU[s][:, 2:TOWN + 2], func=AF.Identity,
                                                    scale=SM[:, C_CW + 12 + j:C_CW + 13 + j]),
                  reads=cuk + ["sm"], writes=[f"y{s}"])
            cx.op(DVE, lambda: nc.vector.scalar_tensor_tensor(
                out=YB[s][:], in0=CU[s][:, 1:TOWN + 1], scalar=SM[:, C_CW + 6 + j:C_CW + 7 + j], in1=YB[s][:],
                op0=ALU.mult, op1=ALU.add), reads=cuk + [f"y{s}", "sm"], writes=[f"y{s}"])
            cx.op(DVE, lambda: nc.vector.scalar_tensor_tensor(
                out=YB[s][:], in0=CU[s][:, 0:TOWN], scalar=SM[:, C_CW + j:C_CW + 1 + j], in1=YB[s][:],
                op0=ALU.mult, op1=ALU.add), reads=cuk + [f"y{s}", "sm"], writes=[f"y{s}"])
            for ti, (t0, tl) in enumerate(TG_OWN):
                b = cx.bank()
                cx.mm_group(b.ap[:, 0:tl], [(view[:, kc, 0:128], HB[:, kc, t0:t0 + tl]) for kc in range(NKC)],
                            reads=[key], writes=[b.key])
                cx.op(DVE, lambda: nc.vector.scalar_tensor_tensor(
                    out=MIX[:, j, t0 - HALO:t0 - HALO + tl], in0=b.ap[:, 0:tl],
                    scalar=SM[:, C_BIN + j:C_BIN + 1 + j], in1=YB[s][:, t0 - HALO:t0 - HALO + tl],
                    op0=ALU.add, op1=ALU.mult), reads=[b.key, f"y{s}", "sm"], writes=[f"mix{j}_{ti}"])
        out_proj(0, 0, 3, wo_pre)
        wo2 = load_wo(3, 3)
        q_pre = load_cols(w_in_v, [(2304, 384)])
        out_proj(3, 3, 3, wo2)
        cx.barrier()

    with ExitStack() as st:
        QTP = sb(st, "QTP", [128, 3, TOWN], BF16)
        KTA = sb(st, "KTA", [128, T], BF16)
        KTB = sb(st, "KTB", [128, T], BF16)
        VA = sb(st, "VA", [128, 9, 2, 128], BF16)
        BIASF = sb(st, "BIASF", [128, 2, 3, 128], F32)
        BIAS0F = sb(st, "BIAS0F", [128, 3, 128], F32)
        BH = [sb(st, f"BH{i}", [128, 2, 3, 128], BF16) for i in range(2)]
        BL = [sb(st, f"BL{i}", [128, 2, 3, 128], BF16) for i in range(2)]
        B0H = [sb(st, f"B0H{i}", [128, 3, 128], BF16) for i in range(2)]
        B0L = [sb(st, f"B0L{i}", [128, 3, 128], BF16) for i in range(2)]
        ES2 = sb(st, "ES2", [64, 12, 128], BF16)
        TH = sb(st, "TH", [64, 12], BF16)
        TL = sb(st, "TL", [64, 12], F32)
        IDB = sb(st, "IDB", [128, 128], BF16)
        EB = [sb(st, f"EB{i}", [128, 2, 384], BF16) for i in range(2)]
        RD = [sb(st, f"RD{i}", [128, 384], F32) for i in range(2)]
        bias_v = biasT_d.rearrange("p (k h q) -> p k h q", k=2, h=12)
        bias0_v = bias0T_d.rearrange("p (h q) -> p h q", h=12)
        cx.dma(PQ, "idb", out=IDB[:], in_=ident_d, writes=["idb"])
        cx.op(DVE, lambda: nc.vector.memset(KTA[64:128, :], 0.0), writes=["kta_z"])
        cx.op(DVE, lambda: nc.vector.memset(KTB[0:64, :], 0.0), writes=["ktb_z"])
        cx.op(DVE, lambda: nc.vector.tensor_copy(out=TH[:], in_=DER[0:64, 16:28]), writes=["th"])
        cx.op(DVE, lambda: nc.vector.tensor_tensor(out=TL[:], in0=DER[0:64, 16:28], in1=TH[:], op=ALU.subtract),
              reads=["th"], writes=["tl"])
        cx.op(DVE, lambda: nc.vector.tensor_scalar(out=ES2[0:32], in0=TH[0:32].unsqueeze(2).to_broadcast([32, 12, 128]),
                                                   scalar1=1.0 / 32, scalar2=None, op0=ALU.mult),
              reads=["th"], writes=["es2a"])
        cx.op(DVE, lambda: nc.vector.tensor_scalar(out=ES2[32:64], in0=TL[32:64].unsqueeze(2).to_broadcast([32, 12, 128]),
                                                   scalar1=1.0 / 32, scalar2=None, op0=ALU.mult),
              reads=["tl"], writes=["es2b"])
        wo_pre = None
        for sg in range(2):
            g0 = 2 * sg
            qview, qkey = q_pre if sg == 0 else load_cols(w_in_v, [(2304 + 384 * sg, 384)])
            kview, kkey = load_cols(w_in_v, [(3072 + 64 * g0, 64), (3072 + 64 * g0, 64),
                                             (3072 + 64 * (g0 + 1), 64), (3072 + 64 * (g0 + 1), 64),
                                             (3328 + 128 * sg, 128)])
            if sg == 1:
                wo_pre = load_wo(6, 3)
            for pr in range(3):
                for ti, (t0, tl) in enumerate(TG_OWN):
                    b = cx.bank()
                    cx.mm_group(b.ap[:, 0:tl],
                                [(qview[:, kc, pr * 128:(pr + 1) * 128], HB[:, kc, t0:t0 + tl]) for kc in range(NKC)],
                                reads=[qkey], writes=[b.key])
                    cx.op(ACT, lambda: nc.scalar.activation(out=QTP[:, pr, t0 - HALO:t0 - HALO + tl], in_=b.ap[:, 0:tl],
                                                            func=AF.Identity, bias=DER[:, 3 * sg + pr:3 * sg + pr + 1],
                                                            scale=0.125),
                          reads=[b.key, "der0"], writes=[f"qt{pr}_{ti}"])
            for blk in range(9):
                b = cx.bank()
                cx.mm_group(b.ap[:, 0:128],
                            [(HB[:, kc, blk * 128:(blk + 1) * 128], kview[:, kc, 256:384]) for kc in range(NKC)],
                            reads=[kkey], writes=[b.key])
                for hf in range(2):
                    cx.op(DVE, lambda: nc.vector.tensor_tensor(
                        out=VA[:, blk, :, hf * 64:(hf + 1) * 64],
                        in0=b.ap[:, 0:128].rearrange("p (g d) -> p g d", g=2),
                        in1=SM[:, C_BV + 128 * sg:C_BV + 128 * sg + 128].rearrange("p (g d) -> p g d", g=2),
                        op=ALU.add), reads=[b.key, "sm"], writes=["va"])

            for gi in range(2):
                g = g0 + gi
                gb = g % 2
                cx.dma(SQ_, "biasf", out=BIASF[:], in_=bias_v[:, :, 3 * g:3 * g + 3, :], writes=["biasf"])
                cx.dma(SQ_, "bias0f", out=BIAS0F[:], in_=bias0_v[:, 3 * g:3 * g + 3, :], writes=["bias0f"])
                cx.op(ACT, lambda: nc.scalar.copy(out=BH[gb][:], in_=BIASF[:]), reads=["biasf"], writes=[f"bh{gb}"])
                cx.op(DVE, lambda: nc.vector.tensor_tensor(out=BL[gb][:], in0=BIASF[:], in1=BH[gb][:], op=ALU.subtract),
                      reads=["biasf", f"bh{gb}"], writes=[f"bl{gb}"])
                cx.op(ACT, lambda: nc.scalar.copy(out=B0H[gb][:], in_=BIAS0F[:]), reads=["bias0f"], writes=[f"b0h{gb}"])
                cx.op(DVE, lambda: nc.vector.tensor_tensor(out=B0L[gb][:], in0=BIAS0F[:], in1=B0H[gb][:],
                                                           op=ALU.subtract),
                      reads=["bias0f", f"b0h{gb}"], writes=[f"b0l{gb}"])
                for (t0, tl) in TG_ALL:
                    b = cx.bank()
                    cx.mm_group(b.ap[:, 0:tl],
                                [(kview[:, kc, gi * 128:(gi + 1) * 128], HB[:, kc, t0:t0 + tl]) for kc in range(NKC)],
                                reads=[kkey], writes=[b.key])
                    cx.op(ACT, lambda: nc.scalar.activation(out=KTA[0:64, t0:t0 + tl], in_=b.ap[0:64, 0:tl],
                                                            func=AF.Identity, bias=SM[0:64, C_BK + g:C_BK + g + 1],
                                                            scale=1.0), reads=[b.key, "sm"], writes=["kta"])
                    cx.op(ACT, lambda: nc.scalar.activation(out=KTB[64:128, t0:t0 + tl], in_=b.ap[64:128, 0:tl],
                                                            func=AF.Identity, bias=SM[64:128, C_BK + g:C_BK + g + 1],
                                                            scale=1.0), reads=[b.key, "sm"], writes=["ktb"])

                def ST(n, c):
                    pt, pk = cx.bank2()
                    for kb in range(2):
                        if n == 0 and kb == 0:
                            hi, lo = B0H[gb][:].rearrange("p h q -> p (h q)"), B0L[gb][:].rearrange("p h q -> p (h q)")
                            bkeys = [f"b0h{gb}", f"b0l{gb}"]
                        else:
                            hi = BH[gb][:, kb].rearrange("p h q -> p (h q)")
                            lo = BL[gb][:, kb].rearrange("p h q -> p (h q)")
                            bkeys = [f"bh{gb}", f"bl{gb}"]
                        rd = bkeys + ["idb", "kta", "ktb", "kta_z", "ktb_z"] + [f"qt{pr}_{n // 4}" for pr in range(3)]
                        cx.deps(PE, rd, [pk[kb]])
                        nc.tensor.matmul(pt[:, kb, 0:384], lhsT=IDB[:], rhs=hi, start=True, stop=False)
                        nc.tensor.matmul(pt[:, kb, 0:384], lhsT=IDB[:], rhs=lo, start=False, stop=False)
                        ins = None
                        for jh in range(3):
                            hs = 3 * gi + jh
                            pr = hs // 2
                            kt = KTA if hs % 2 == 0 else KTB
                            ins = nc.tensor.matmul(pt[:, kb, jh * 128:(jh + 1) * 128],
                                                   lhsT=kt[:, (n + kb) * 128:(n + kb + 1) * 128],
                                                   rhs=QTP[:, pr, n * 128:(n + 1) * 128],
                                                   start=False, stop=(jh == 2))
                        ins.then_inc(PE.sem.h, 1)
                        PE.sem.count += 1
                        cx.trace.setdefault("pe", []).append(("i", id(PE.sem), 1))
                        cx.commit((PE.sem, PE.sem.count), rd, [pk[kb]])
                    cx.op(ACT, lambda: nc.scalar.activation(out=EB[c][:], in_=pt[:, :, 0:384], func=AF.Exp),
                          reads=pk, writes=[f"eb{c}"])

                def PV(n, c):
                    bn = cx.bank(); bd = cx.bank()
                    cx.mm_group(bn.ap[:, 0:384], [(VA[:, n + kb, gi, :], EB[c][:, kb, :]) for kb in range(2)],
                                reads=["va", f"eb{c}"], writes=[bn.key])
                    cx.mm_group(bd.ap[:, 0:384],
                                [(ONESB[:], EB[c][:, kb, :]) for kb in range(2)]
                                + [(ONESB[0:64, :], ES2[:, 3 * g:3 * g + 3, :])],
                                reads=["onesb", f"eb{c}", "es2a", "es2b"], writes=[bd.key])
                    cx.op(DVE, lambda: nc.vector.reciprocal(out=RD[c][:], in_=bd.ap[:, 0:384]), reads=[bd.key],
                          writes=[f"rd{c}"])
                    for jh in range(3):
                        h = 3 * g + jh
                        p0 = (h % 2) * 64
                        cx.op(DVE, lambda: nc.vector.tensor_tensor(
                            out=MIX[p0:p0 + 64, h // 2, n * 128:(n + 1) * 128],
                            in0=bn.ap[p0:p0 + 64, jh * 128:(jh + 1) * 128], in1=RD[c][p0:p0 + 64, jh * 128:(jh + 1) * 128],
                            op=ALU.mult), reads=[bn.key, f"rd{c}"], writes=[f"mix{h // 2}_{n // 4}"])

                ST(0, 0)
                for n in range(8):
                    if n + 1 < 8:
                        ST(n + 1, (n + 1) % 2)
                    PV(n, n % 2)
        out_proj(0, 6, 3, wo_pre)
        wo2 = load_wo(9, 3)
        qm_pre = load_cols(w_in_v, [(3584, 256)])
        out_proj(3, 9, 3, wo2)
        cx.barrier()

    with ExitStack() as st:
        QM = [sb(st, f"QM{i}", [128, TOWN], BF16) for i in range(2)]
        E2 = [sb(st, f"E2{i}", [128, 2, 512], BF16) for i in range(2)]
        RD2 = [sb(st, f"RD2{i}", [128, 512], F32) for i in range(2)]
        cnt = 0
        wo_pre = None
        for hp in range(2):
            view, key = qm_pre if hp == 0 else load_cols(w_in_v, [(3584 + 256 * hp, 256)])
            if hp == 1:
                wo_pre = load_wo(12, 2)
            for hh in range(2):
                h = 2 * hp + hh
                qs = h % 2
                for ti, (t0, tl) in enumerate(TG_OWN):
                    b = cx.bank()
                    cx.mm_group(b.ap[:, 0:tl],
                                [(view[:, kc, hh * 128:(hh + 1) * 128], HB[:, kc, t0:t0 + tl]) for kc in range(NKC)],
                                reads=[key], writes=[b.key])
                    cx.op(ACT, lambda: nc.scalar.activation(out=QM[qs][:, t0 - HALO:t0 - HALO + tl], in_=b.ap[:, 0:tl],
                                                            func=AF.Identity, bias=DER[:, 12 + h:13 + h],
                                                            scale=128.0 ** -0.5),
                          reads=[b.key, "der1"], writes=[f"qm{qs}_{ti}"])
                for ti, (t0, tl) in enumerate(TG_OWN):
                    c = cnt % 2
                    cnt += 1
                    pt, pk = cx.bank2()
                    for mc in range(2):
                        cx.mm_group(pt[:, mc, 0:tl],
                                    [(MEMK[:, h, mc * 128:(mc + 1) * 128], QM[qs][:, t0 - HALO:t0 - HALO + tl])],
                                    reads=[f"qm{qs}_{ti}"], writes=[pk[mc]])
                    cx.op(ACT, lambda: nc.scalar.activation(out=E2[c][:], in_=pt[:], func=AF.Exp),
                          reads=pk, writes=[f"e2{c}"])
                    bn = cx.bank(); bd = cx.bank()
                    cx.mm_group(bn.ap[:, 0:tl],
                                [(MEMV[:, mc, h * 128:(h + 1) * 128], E2[c][:, mc, :]) for mc in range(2)],
                                reads=[f"e2{c}"], writes=[bn.key])
                    cx.mm_group(bd.ap[:, 0:tl], [(ONESB[:], E2[c][:, mc, :]) for mc in range(2)],
                                reads=[f"e2{c}", "onesb"], writes=[bd.key])
                    cx.op(DVE, lambda: nc.vector.reciprocal(out=RD2[c][:], in_=bd.ap[:, 0:tl]), reads=[bd.key],
                          writes=[f"rd2{c}"])
                    cx.op(DVE, lambda: nc.vector.tensor_tensor(out=MIX[:, h, t0 - HALO:t0 - HALO + tl],
                                                               in0=bn.ap[:, 0:tl], in1=RD2[c][:], op=ALU.mult),
                          reads=[bn.key, f"rd2{c}"], writes=[f"mix{h}_{ti}"])
        out_proj(0, 12, 2, wo_pre)
        out_proj(2, 14, 2)
        cx.barrier()
    p2.close()

    f2 = FFN("f2", w2g, w2u, w2d, TG_OWN, lambda ti: [f"hb{d}_{ti}" for d in range(NKC)], ln_prev=1)
    f2.prefetch()
    layernorm("l2", TG_OWN, 1, False)
    f2.run()
    layernorm("l3", TG_OWN, 2, True)

    nc.sync.wait_ge(osem.h, osem.count)
    es.close()
    nc._cx_trace = cx.trace if hasattr(nc, "__dict__") else None
    build_program.last_trace = cx.trace
    return nc


def _t5_bucket(dist):
    max_exact = 16
    d = np.maximum(dist, 1).astype(np.float64)
    large = max_exact + (np.log(d / max_exact) / math.log(128 / max_exact) * (32 - max_exact)).astype(np.int32)
    large = np.minimum(large, 31)
    return np.where(dist < max_exact, dist, large)


_NC_CACHE = {}


def kernel(x, mem, ln1_g, ln1_b, ffn1_w_gate, ffn1_w_up, ffn1_w_down, w_in, b_in, conv_w, sinks, w_mem_kv,
           w_out, ln2_g, ln2_b, ffn2_w_gate, ffn2_w_up, ffn2_w_down, ln3_g, ln3_b, rel_bias):
    f32 = np.float32
    x = np.asarray(x, f32); mem = np.asarray(mem, f32)
    b_in0 = np.asarray(b_in, f32)[0]
    rel_bias = np.asarray(rel_bias, f32)

    sm = np.zeros((128, NS), f32)
    for i, v in enumerate([ln1_g, ln1_b, ln2_g, ln2_b, ln3_g, ln3_b]):
        sm[:, C_LN + 16 * i:C_LN + 16 * (i + 1)] = np.asarray(v, f32)[0].reshape(16, 128).T
    sm[:, C_BIN:C_BIN + 32] = b_in0.reshape(32, 128).T
    sm[0:64, C_BQ:C_BQ + 12] = b_in0[2304:3072].reshape(12, 64).T
    sm[:, C_BK:C_BK + 4] = np.tile(b_in0[3072:3328].reshape(4, 64).T, (2, 1))
    cw = np.asarray(conv_w, f32)[0]
    for tap in range(3):
        sm[:, C_CW + 6 * tap:C_CW + 6 * (tap + 1)] = cw[tap].reshape(6, 128).T
    sm[:, C_SINK:C_SINK + 12] = np.asarray(sinks, f32)[0][None, :]
    sm[:, C_EPS] = LN_EPS
    sm[:, C_BV:C_BV + 256] = b_in0[3328:3584][None, :]

    kk = np.arange(128)[:, None]
    qq = np.arange(128)[None, :]
    bias_t = np.full((128, 2, 12, 128), NEG, f32)
    for kb in range(2):
        dist = qq + 128 - (kk + 128 * kb)
        ok = (dist >= 0) & (dist < 128)
        bk = _t5_bucket(np.maximum(dist, 0))
        vals = rel_bias[bk]
        bias_t[:, kb] = np.where(ok[:, None, :], vals.transpose(0, 2, 1), f32(NEG))
    bias_t_flat = np.ascontiguousarray(bias_t.reshape(128, -1))
    bias0_valid = np.ascontiguousarray(bias_t[:, 0].reshape(128, -1))
    bias0_masked = np.full_like(bias0_valid, NEG)

    shared = {
        "w1g": np.ascontiguousarray(np.asarray(ffn1_w_gate, f32)[0]),
        "w1u": np.ascontiguousarray(np.asarray(ffn1_w_up, f32)[0]),
        "w1d": np.ascontiguousarray(np.asarray(ffn1_w_down, f32)[0]),
        "w2g": np.ascontiguousarray(np.asarray(ffn2_w_gate, f32)[0]),
        "w2u": np.ascontiguousarray(np.asarray(ffn2_w_up, f32)[0]),
        "w2d": np.ascontiguousarray(np.asarray(ffn2_w_down, f32)[0]),
        "w_in": np.ascontiguousarray(np.asarray(w_in, f32)[0]),
        "w_mkv": np.ascontiguousarray(np.asarray(w_mem_kv, f32)[0]),
        "w_out": np.ascontiguousarray(np.asarray(w_out, f32)[0]),
        "biasT": bias_t_flat,
        "ident": np.eye(128, dtype=f32),
    }
    in_maps = []
    for c in range(8):
        b, half = c // 2, c % 2
        own = x[b, half * TOWN:(half + 1) * TOWN]
        halo = x[b, TOWN - HALO:TOWN] if half == 1 else np.zeros((HALO, D), f32)
        xt = np.ascontiguousarray(np.concatenate([halo, own], axis=0).T)
        smc = sm.copy()
        smc[:, C_FLAG] = float(half)
        m = dict(shared)
        m["xT"] = xt
        m["memT"] = np.ascontiguousarray(mem[b].T)
        m["smalls"] = smc
        m["bias0T"] = bias0_valid if half == 1 else bias0_masked
        in_maps.append(m)

    if "nc" not in _NC_CACHE:
        _NC_CACHE["nc"] = build_program()
    nc = _NC_CACHE["nc"]
    res = run_bass_kernel_spmd(nc, in_maps, core_ids=list(range(8)))
    out = np.empty((BATCH, SEQ, D), f32)
    for c in range(8):
        b, half = c // 2, c % 2
        out[b, half * TOWN:(half + 1) * TOWN] = np.asarray(res.results[c]["outT"], f32).T
    return out
```
